# Optimizing a Trainium2 kernel written in Bass

```python
import math
import jax, jax.numpy as jnp
from jax import lax
import numpy as np

D_MODEL = 1024
BATCH = 8
SEQ = 2048
DEPTH = 4
DEC_BATCH = 128
DEC_SEQ = 1
PAST_LEN = 2048
PAGE_SIZE = 128

N_MIXERS = 4
N_LAYERS_A = (DEPTH + 3) // 4
N_LAYERS_B = (DEPTH + 2) // 4
N_LAYERS_C = (DEPTH + 1) // 4
N_LAYERS_D = DEPTH // 4

RMS_EPS = 1e-6
LN_EPS = 1e-5
GM_CHUNK = 128
GM_WIDTH = D_MODEL
GM_GROUPS = 8
GM_GROUP_DIM = GM_WIDTH // GM_GROUPS
MOBA_HEADS = 16
MOBA_HEAD_DIM = D_MODEL // MOBA_HEADS
MOBA_BLOCK = 256
MOBA_TOPK = 3
MOBA_QBLOCK = 64
REL_BUCKETS = 32
REL_MAX_DIST = 128
POOL_WINDOWS = (2, 4, 8, 16)
POOL_GROUPS = len(POOL_WINDOWS)
POOL_GROUP_DIM = D_MODEL // POOL_GROUPS
POOL_CTX = max(POOL_WINDOWS) - 1
RWKV_HEAD_DIM = 64
RWKV_HEADS = D_MODEL // RWKV_HEAD_DIM
RWKV_LORA_W = 64
RWKV_LORA_A = 64
RWKV_LORA_G = 128
RWKV_LNX_EPS = 64e-5
FFN_HIDDEN = 2816
FFN_CONV = 3

kernel_name = 'hybrid_gmlp_moba_pool_rwkv7_step'


def rmsnorm(x, g):
    xf = x.astype(jnp.float32)
    y = xf * lax.rsqrt(jnp.mean(xf * xf, axis=-1, keepdims=True) + RMS_EPS)
    return (y * g.astype(jnp.float32)).astype(x.dtype)


def layernorm(x, g, b, eps):
    xf = x.astype(jnp.float32)
    mu = jnp.mean(xf, axis=-1, keepdims=True)
    var = jnp.mean(jnp.square(xf - mu), axis=-1, keepdims=True)
    return ((xf - mu) * lax.rsqrt(var + eps) * g + b).astype(x.dtype)


def chunk_gmlp(h, w_in, ln_g, ln_b, w_s, b_s, w_out):
    n, t, _ = h.shape
    z = jax.nn.gelu(h @ w_in)
    u, v = jnp.split(z, 2, axis=-1)
    v = layernorm(v, ln_g, ln_b, LN_EPS)
    n_chunks = -(-t // GM_CHUNK)
    vp = jnp.pad(v, ((0, 0), (0, n_chunks * GM_CHUNK - t), (0, 0)))
    vc = vp.reshape(n, n_chunks, GM_CHUNK, GM_GROUPS, GM_GROUP_DIM)
    w_causal = w_s * jnp.tril(jnp.ones((GM_CHUNK, GM_CHUNK), w_s.dtype))
    s = jnp.einsum('gts,ncsgd->nctgd', w_causal, vc) + b_s.T[None, None, :, :, None]
    s = s.reshape(n, n_chunks * GM_CHUNK, GM_WIDTH)[:, :t]
    return (u * s) @ w_out, v


def t5_bucket(rel):
    n = jnp.maximum(rel, 0)
    exact = REL_BUCKETS // 2
    nf = jnp.maximum(n, 1).astype(jnp.float32)
    large = exact + (jnp.log(nf / exact) / math.log(REL_MAX_DIST / exact)
                     * (REL_BUCKETS - exact)).astype(jnp.int32)
    return jnp.where(n < exact, n, jnp.minimum(large, REL_BUCKETS - 1))


def pad_to_blocks(x):
    pad = (-x.shape[-3]) % MOBA_BLOCK
    return jnp.pad(x, [(0, 0)] * (x.ndim - 3) + [(0, pad), (0, 0), (0, 0)])


def moba_core(q, q_pos, k, v, rel_bias):
    f32 = jnp.float32
    nb = k.shape[0] // MOBA_BLOCK
    top = min(MOBA_TOPK, nb)
    kb = k.reshape(nb, MOBA_BLOCK, MOBA_HEADS, MOBA_HEAD_DIM)
    vb = v.reshape(nb, MOBA_BLOCK, MOBA_HEADS, MOBA_HEAD_DIM)
    k_mean = jnp.mean(kb.astype(f32), axis=1)
    own = q_pos // MOBA_BLOCK
    gate = jnp.einsum('qhd,nhd->qhn', q.astype(f32), k_mean)
    past = jnp.arange(nb)[None, None, :] < own[:, None, None]
    gate = jnp.where(past, gate, -jnp.inf)
    _, sel = lax.top_k(gate, top)
    sel_ok = sel < own[:, None, None]
    own_b = jnp.broadcast_to(own[:, None, None], sel.shape[:2] + (1,)).astype(sel.dtype)
    blocks = jnp.concatenate([sel, own_b], axis=-1)
    ok = jnp.concatenate([sel_ok, jnp.ones(sel.shape[:2] + (1,), bool)], axis=-1)
    h_idx = jnp.arange(MOBA_HEADS)[None, :, None]
    k_sel = kb.transpose(2, 0, 1, 3)[h_idx, blocks]
    v_sel = vb.transpose(2, 0, 1, 3)[h_idx, blocks]
    key_pos = blocks[..., None] * MOBA_BLOCK + jnp.arange(MOBA_BLOCK, dtype=blocks.dtype)
    rel = q_pos[:, None, None, None] - key_pos
    bias = rel_bias[t5_bucket(rel), h_idx[..., None]].astype(f32)
    logits = jnp.einsum('qhd,qhjsd->qhjs', q, k_sel).astype(f32) * (MOBA_HEAD_DIM ** -0.5) + bias
    logits = jnp.where(ok[..., None] & (rel >= 0), logits, -jnp.inf)
    p = jax.nn.softmax(logits, axis=(-2, -1))
    return jnp.einsum('qhjs,qhjsd->qhd', p.astype(v.dtype), v_sel)


def moba_qkv(h, w_qkv):
    n, t, _ = h.shape
    z = (h @ w_qkv).reshape(n, t, 3, MOBA_HEADS, MOBA_HEAD_DIM)
    return z[:, :, 0], z[:, :, 1], z[:, :, 2]


def moba_prompt(q, k, v, rel_bias):
    b, t = q.shape[:2]
    nq = t // MOBA_QBLOCK
    kp, vp = pad_to_blocks(k), pad_to_blocks(v)
    qc = q.reshape(b * nq, MOBA_QBLOCK, MOBA_HEADS, MOBA_HEAD_DIM)
    ids = jnp.arange(b * nq, dtype=jnp.int32)
    seq_idx = ids // nq
    pos = (ids % nq)[:, None] * MOBA_QBLOCK + jnp.arange(MOBA_QBLOCK, dtype=jnp.int32)[None, :]
    out = lax.map(lambda a: moba_core(a[0], a[2], kp[a[1]], vp[a[1]], rel_bias), (qc, seq_idx, pos))
    return out.reshape(b, t, MOBA_HEADS, MOBA_HEAD_DIM)


def moba_sample(q, k_new, v_new, cache_k, cache_v, layer, page_table, rel_bias):
    t = q.shape[1]
    pos = PAST_LEN + jnp.arange(t, dtype=jnp.int32)

    def one(a):
        pt, qs, ks, vs = a
        k_past = cache_k[layer, pt].reshape(-1, MOBA_HEADS, MOBA_HEAD_DIM)
        v_past = cache_v[layer, pt].reshape(-1, MOBA_HEADS, MOBA_HEAD_DIM)
        k_all = pad_to_blocks(jnp.concatenate([k_past, ks.astype(k_past.dtype)], axis=0))
        v_all = pad_to_blocks(jnp.concatenate([v_past, vs.astype(v_past.dtype)], axis=0))
        return moba_core(qs, pos, k_all, v_all, rel_bias)

    return lax.map(one, (page_table, q, k_new, v_new))


def pool_mixer(h, prev, start, w, scale):
    f32 = jnp.float32
    n, t, _ = h.shape
    xp = jnp.concatenate([prev.astype(h.dtype), h], axis=1)
    cs = jnp.cumsum(jnp.pad(xp.astype(f32), ((0, 0), (1, 0), (0, 0))), axis=1)
    n_avail = start + jnp.arange(t, dtype=jnp.int32) + 1
    hf = h.astype(f32)
    outs = []
    for g, win in enumerate(POOL_WINDOWS):
        lo, hi = g * POOL_GROUP_DIM, (g + 1) * POOL_GROUP_DIM
        wsum = (cs[:, POOL_CTX + 1:POOL_CTX + 1 + t, lo:hi]
                - cs[:, POOL_CTX + 1 - win:POOL_CTX + 1 - win + t, lo:hi])
        cnt = jnp.minimum(n_avail, win).astype(f32)[None, :, None]
        outs.append((wsum / cnt - hf[..., lo:hi]) @ w[g])
    y = jnp.concatenate(outs, axis=-1) * scale
    return y.astype(h.dtype), xp[:, -POOL_CTX:]


def rwkv7_mixer(h, shift_prev, wkv_prev, mu, w_r, w_k, w_v, w_o, w0, w1, w2,
                a0, a1, a2, g1, g2, k_k, k_a, r_k, lnx_g, lnx_b):
    f32 = jnp.float32
    n, t, d = h.shape
    hf = h.astype(f32)
    xx = jnp.concatenate([shift_prev[:, None].astype(f32), hf[:, :-1]], axis=1) - hf
    mu = mu.astype(f32)
    xr, xw, xk, xv, xa, xg = [hf + xx * mu[m] for m in range(6)]
    r = xr @ w_r
    k = xk @ w_k
    v = xv @ w_v
    w_log = -jax.nn.softplus(-(w0 + jnp.tanh(xw @ w1) @ w2)) - 0.5
    decay = jnp.exp(-jnp.exp(w_log))
    a = jax.nn.sigmoid(a0 + (xa @ a1) @ a2)
    g = jax.nn.sigmoid(xg @ g1) @ g2

    def heads(z):
        return z.reshape(n, t, RWKV_HEADS, RWKV_HEAD_DIM)

    kk = heads(k * k_k)
    kk = kk / jnp.maximum(jnp.linalg.norm(kk, axis=-1, keepdims=True), 1e-12)
    k = k * (1.0 + (a - 1.0) * k_a)
    r, k, v, decay, a = heads(r), heads(k), heads(v), heads(decay), heads(a)

    def step(S, inp):
        r_t, w_t, k_t, v_t, kk_t, a_t = inp
        sa = jnp.einsum('nhij,nhj->nhi', S, -kk_t)
        S = (S * w_t[:, :, None, :] + sa[..., None] * (kk_t * a_t)[:, :, None, :]
             + v_t[..., None] * k_t[:, :, None, :])
        return S, jnp.einsum('nhij,nhj->nhi', S, r_t)

    xs = tuple(jnp.swapaxes(z, 0, 1) for z in (r, decay, k, v, kk, a))
    S, o = lax.scan(step, wkv_prev.astype(f32), xs)
    o = jnp.swapaxes(o, 0, 1)
    o = layernorm(o, lnx_g.reshape(RWKV_HEADS, RWKV_HEAD_DIM),
                  lnx_b.reshape(RWKV_HEADS, RWKV_HEAD_DIM), RWKV_LNX_EPS)
    o = o + jnp.sum(r * k * r_k, axis=-1, keepdims=True) * v
    y = (o.reshape(n, t, d) * g) @ w_o
    return y.astype(h.dtype), h[:, -1], S


def conv_ffn(h, prev, w_in, conv_w, conv_b, w_out):
    t = h.shape[1]
    gate, up = jnp.split(h @ w_in, 2, axis=-1)
    gp = jnp.concatenate([prev.astype(gate.dtype), gate], axis=1)
    c = sum((conv_w[j] * gp[:, j:j + t] for j in range(FFN_CONV)), conv_b)
    return (jax.nn.gelu(c) * up) @ w_out, gp[:, -(FFN_CONV - 1):]


def _keys(key):
    i = 0
    while True:
        yield jax.random.fold_in(key, i)
        i += 1


def setup_inputs(seed: int = 0) -> dict:
    key = jax.random.key(seed)
    ks = _keys(key)
    f32 = jnp.float32

    def nrm(shape, scale=1.0):
        return jax.random.normal(next(ks), shape, f32) * scale

    def unif(shape, lo, hi):
        return jax.random.uniform(next(ks), shape, f32, lo, hi)

    D = D_MODEL
    H, Dh = MOBA_HEADS, MOBA_HEAD_DIM
    RH, RN = RWKV_HEADS, RWKV_HEAD_DIM
    n_pages = PAST_LEN // PAGE_SIZE
    n_pool = (DEC_BATCH * n_pages * 5) // 4
    inp = {}
    inp['x_prompt'] = nrm((BATCH, SEQ, D))
    inp['x_sample'] = nrm((DEC_BATCH, DEC_SEQ, D))
    inp['cache_moba_k'] = nrm((N_LAYERS_B, n_pool, PAGE_SIZE, H, Dh))
    inp['cache_moba_v'] = nrm((N_LAYERS_B, n_pool, PAGE_SIZE, H, Dh))
    inp['state_pool'] = nrm((N_LAYERS_C, DEC_BATCH, POOL_CTX, D))
    inp['state_rwkv_wkv'] = nrm((N_LAYERS_D, DEC_BATCH, RH, RN, RN), 0.3)
    inp['state_rwkv_shift'] = nrm((N_LAYERS_D, DEC_BATCH, D))
    inp['state_ffn_conv'] = nrm((DEPTH, DEC_BATCH, FFN_CONV - 1, FFN_HIDDEN))
    perm = jax.random.permutation(next(ks), n_pool)[:DEC_BATCH * n_pages]
    inp['page_table'] = perm.reshape(DEC_BATCH, n_pages).astype(jnp.int32)
    inp['norm_mix_g'] = 1.0 + nrm((DEPTH, D), 0.05)
    inp['norm_ffn_g'] = 1.0 + nrm((DEPTH, D), 0.05)
    inp['norm_final_g'] = 1.0 + nrm((D,), 0.05)
    inp['rel_bias'] = nrm((REL_BUCKETS, H), 0.3)
    inp['gm_w_in'] = nrm((N_LAYERS_A, D, 2 * GM_WIDTH), D ** -0.5)
    inp['gm_ln_g'] = 1.0 + nrm((N_LAYERS_A, GM_WIDTH), 0.05)
    inp['gm_ln_b'] = nrm((N_LAYERS_A, GM_WIDTH), 0.02)
    inp['gm_w_s'] = nrm((N_LAYERS_A, GM_GROUPS, GM_CHUNK, GM_CHUNK), 0.5 * GM_CHUNK ** -0.5)
    inp['gm_b_s'] = 1.0 + nrm((N_LAYERS_A, GM_GROUPS, GM_CHUNK), 0.05)
    inp['gm_w_out'] = nrm((N_LAYERS_A, GM_WIDTH, D), GM_WIDTH ** -0.5)
    inp['moba_w_qkv'] = nrm((N_LAYERS_B, D, 3 * D), D ** -0.5)
    inp['moba_w_o'] = nrm((N_LAYERS_B, D, D), D ** -0.5)
    inp['pool_w'] = nrm((N_LAYERS_C, POOL_GROUPS, POOL_GROUP_DIM, POOL_GROUP_DIM), POOL_GROUP_DIM ** -0.5)
    inp['pool_scale'] = 0.5 + nrm((N_LAYERS_C, D), 0.05)
    inp['rwkv_mu'] = unif((N_LAYERS_D, 6, D), 0.0, 1.0)
    inp['rwkv_w_r'] = nrm((N_LAYERS_D, D, D), D ** -0.5)
    inp['rwkv_w_k'] = nrm((N_LAYERS_D, D, D), D ** -0.5)
    inp['rwkv_w_v'] = nrm((N_LAYERS_D, D, D), D ** -0.5)
    inp['rwkv_w_o'] = nrm((N_LAYERS_D, D, D), D ** -0.5)
    inp['rwkv_w0'] = unif((N_LAYERS_D, D), -4.0, 0.0)
    inp['rwkv_w1'] = nrm((N_LAYERS_D, D, RWKV_LORA_W), D ** -0.5)
    inp['rwkv_w2'] = nrm((N_LAYERS_D, RWKV_LORA_W, D), 0.5 * RWKV_LORA_W ** -0.5)
    inp['rwkv_a0'] = nrm((N_LAYERS_D, D), 0.1)
    inp['rwkv_a1'] = nrm((N_LAYERS_D, D, RWKV_LORA_A), D ** -0.5)
    inp['rwkv_a2'] = nrm((N_LAYERS_D, RWKV_LORA_A, D), RWKV_LORA_A ** -0.5)
    inp['rwkv_g1'] = nrm((N_LAYERS_D, D, RWKV_LORA_G), D ** -0.5)
    inp['rwkv_g2'] = nrm((N_LAYERS_D, RWKV_LORA_G, D), RWKV_LORA_G ** -0.5)
    inp['rwkv_k_k'] = 0.85 + nrm((N_LAYERS_D, D), 0.05)
    inp['rwkv_k_a'] = 1.0 + nrm((N_LAYERS_D, D), 0.05)
    inp['rwkv_r_k'] = nrm((N_LAYERS_D, RH, RN), 0.1)
    inp['rwkv_lnx_g'] = 1.0 + nrm((N_LAYERS_D, D), 0.05)
    inp['rwkv_lnx_b'] = nrm((N_LAYERS_D, D), 0.02)
    inp['ffn_w_in'] = nrm((DEPTH, D, 2 * FFN_HIDDEN), D ** -0.5)
    inp['ffn_conv_w'] = nrm((DEPTH, FFN_CONV, FFN_HIDDEN), FFN_CONV ** -0.5)
    inp['ffn_conv_b'] = nrm((DEPTH, FFN_HIDDEN), 0.02)
    inp['ffn_w_out'] = nrm((DEPTH, FFN_HIDDEN, D), FFN_HIDDEN ** -0.5)
    return inp


def reference(x_prompt, x_sample, cache_moba_k, cache_moba_v, state_pool, state_rwkv_wkv,
              state_rwkv_shift, state_ffn_conv, page_table, norm_mix_g, norm_ffn_g, norm_final_g,
              rel_bias, gm_w_in, gm_ln_g, gm_ln_b, gm_w_s, gm_b_s, gm_w_out, moba_w_qkv, moba_w_o,
              pool_w, pool_scale, rwkv_mu, rwkv_w_r, rwkv_w_k, rwkv_w_v, rwkv_w_o, rwkv_w0, rwkv_w1,
              rwkv_w2, rwkv_a0, rwkv_a1, rwkv_a2, rwkv_g1, rwkv_g2, rwkv_k_k, rwkv_k_a, rwkv_r_k,
              rwkv_lnx_g, rwkv_lnx_b, ffn_w_in, ffn_conv_w, ffn_conv_b, ffn_w_out):
    bp, tp, D = x_prompt.shape
    bs, ts, _ = x_sample.shape
    yp, ys = x_prompt, x_sample
    gm_v_s = []
    k_p, v_p, k_s, v_s = [], [], [], []
    pool_p, pool_s = [], []
    wkv_p, wkv_s, sh_p, sh_s = [], [], [], []
    conv_p, conv_s = [], []
    for i in range(DEPTH):
        j = i // N_MIXERS
        kind = i % N_MIXERS
        hp = rmsnorm(yp, norm_mix_g[i])
        hs = rmsnorm(ys, norm_mix_g[i])
        if kind == 0:
            gm = (gm_w_in[j], gm_ln_g[j], gm_ln_b[j], gm_w_s[j], gm_b_s[j], gm_w_out[j])
            mp, _ = chunk_gmlp(hp, *gm)
            ms, vrow = chunk_gmlp(hs, *gm)
            gm_v_s.append(vrow)
        elif kind == 1:
            qp, kp, vp = moba_qkv(hp, moba_w_qkv[j])
            qs, kss, vss = moba_qkv(hs, moba_w_qkv[j])
            op = moba_prompt(qp, kp, vp, rel_bias)
            os_ = moba_sample(qs, kss, vss, cache_moba_k, cache_moba_v, j, page_table, rel_bias)
            mp = op.reshape(bp, tp, D) @ moba_w_o[j]
            ms = os_.reshape(bs, ts, D) @ moba_w_o[j]
            k_p.append(kp)
            v_p.append(vp)
            k_s.append(kss)
            v_s.append(vss)
        elif kind == 2:
            mp, st_p = pool_mixer(hp, jnp.zeros((bp, POOL_CTX, D), hp.dtype), 0, pool_w[j], pool_scale[j])
            ms, st_s = pool_mixer(hs, state_pool[j], PAST_LEN, pool_w[j], pool_scale[j])
            pool_p.append(st_p)
            pool_s.append(st_s)
        else:
            rw = (rwkv_mu[j], rwkv_w_r[j], rwkv_w_k[j], rwkv_w_v[j], rwkv_w_o[j], rwkv_w0[j],
                  rwkv_w1[j], rwkv_w2[j], rwkv_a0[j], rwkv_a1[j], rwkv_a2[j], rwkv_g1[j], rwkv_g2[j],
                  rwkv_k_k[j], rwkv_k_a[j], rwkv_r_k[j], rwkv_lnx_g[j], rwkv_lnx_b[j])
            mp, shp, Sp = rwkv7_mixer(hp, jnp.zeros((bp, D), hp.dtype),
                                      jnp.zeros((bp, RWKV_HEADS, RWKV_HEAD_DIM, RWKV_HEAD_DIM), jnp.float32), *rw)
            ms, shs, Ss = rwkv7_mixer(hs, state_rwkv_shift[j], state_rwkv_wkv[j], *rw)
            wkv_p.append(Sp)
            wkv_s.append(Ss)
            sh_p.append(shp)
            sh_s.append(shs)
        yp = yp + mp
        ys = ys + ms
        ff = (ffn_w_in[i], ffn_conv_w[i], ffn_conv_b[i], ffn_w_out[i])
        fp, cp = conv_ffn(rmsnorm(yp, norm_ffn_g[i]), jnp.zeros((bp, FFN_CONV - 1, FFN_HIDDEN), yp.dtype), *ff)
        fs, cs = conv_ffn(rmsnorm(ys, norm_ffn_g[i]), state_ffn_conv[i], *ff)
        conv_p.append(cp)
        conv_s.append(cs)
        yp = yp + fp
        ys = ys + fs
    y_prompt = rmsnorm(yp, norm_final_g)
    y_sample = rmsnorm(ys, norm_final_g)
    gm_v_sample = jnp.stack(gm_v_s)
    moba_k_prompt = jnp.stack(k_p)
    moba_v_prompt = jnp.stack(v_p)
    moba_k_sample = jnp.stack(k_s)
    moba_v_sample = jnp.stack(v_s)
    pool_prompt = jnp.stack(pool_p)
    pool_sample = jnp.stack(pool_s)
    wkv_prompt = jnp.stack(wkv_p)
    wkv_sample = jnp.stack(wkv_s)
    shift_prompt = jnp.stack(sh_p)
    shift_sample = jnp.stack(sh_s)
    conv_prompt = jnp.stack(conv_p)
    conv_sample = jnp.stack(conv_s)
    return (y_prompt, y_sample, gm_v_sample, moba_k_prompt, moba_v_prompt, moba_k_sample,
            moba_v_sample, pool_prompt, pool_sample, wkv_prompt, wkv_sample, shift_prompt,
            shift_sample, conv_prompt, conv_sample)
```

```python
import numpy as np
import concourse.bass as bass
import concourse.mybir as mybir
from concourse.bass_utils import run_bass_kernel_spmd

F32 = mybir.dt.float32
BF16 = mybir.dt.bfloat16
I32 = mybir.dt.int32
U32 = mybir.dt.uint32
AF = mybir.ActivationFunctionType
ALU = mybir.AluOpType
AX = mybir.AxisListType

_WRITE_KW = ("out", "accum_out", "ap", "out_ap")


def _region(ap):
    t = ap.tensor
    name = t.name
    pairs = list(ap.ap)
    off = int(ap.offset)
    sp = str(ap.space) if hasattr(ap, "space") else ""
    if "DRAM" in sp.upper() or "HBM" in sp.upper() or type(t).__name__.startswith("DRam"):
        lo = off
        hi = off + sum((int(c) - 1) * abs(int(s)) for s, c in pairs) + 1
        return [(name, 0, 1, lo, hi)]
    if type(t).__name__.startswith("PSum"):
        return [(name, 0, 128, 0, 1 << 30)]
    shape = [int(s) for s in t.shape]
    pstride = 1
    for s in shape[1:]:
        pstride *= s
    p0 = off // pstride
    f0 = off % pstride
    pc = int(pairs[0][1])
    pstep = int(pairs[0][0])
    if pstep == 0:
        pc = 1
    esz = mybir.dt.size(ap.dtype)
    f0 *= esz
    fp = [(abs(int(s)) * esz, int(c)) for s, c in pairs[1:] if int(c) > 1]
    fp.append((1, esz))
    if len(fp) >= 3:
        fp.sort(reverse=True)
        (s0, c0) = fp[0]
        inner = sum((c - 1) * s for s, c in fp[1:]) + 1
        if c0 <= 64 and inner <= s0 and c0 > 1:
            return [(name, p0, p0 + pc, f0 + k * s0, f0 + k * s0 + inner) for k in range(c0)]
    hi = f0 + sum((c - 1) * s for s, c in fp) + 1
    return [(name, p0, p0 + pc, f0, hi)]


def _ovl(a, b):
    return a[0] == b[0] and a[1] < b[2] and b[1] < a[2] and a[3] < b[4] and b[3] < a[4]


class Sched:
    ENG = ("pe", "dve", "act", "pool", "sp")

    def __init__(self, nc, n_dma_sems=40):
        self.nc = nc
        self.stream = {e: [] for e in self.ENG}
        self.cnt = {e: 0 for e in self.ENG}
        self.known = {e: {} for e in self.ENG}
        self.recs = {}
        self.n_dma_sems = n_dma_sems
        self.dma_tgt = [0] * n_dma_sems
        self.dma_rr = 0
        self.sems = {}
        self.n_instr = 0
        self.pd_n = 8
        self.pd_rr = 0
        self.pd_tgt = [0] * self.pd_n

    def _deps(self, eng, reads, writes):
        need = {}
        for r in reads:
            for (reg, sk, val, isw) in self.recs.get(r[0], ()):
                if isw and _ovl(reg, r):
                    if need.get(sk, 0) < val:
                        need[sk] = val
        for w in writes:
            for (reg, sk, val, isw) in self.recs.get(w[0], ()):
                if _ovl(reg, w):
                    if need.get(sk, 0) < val:
                        need[sk] = val
        waits = []
        kn = self.known[eng]
        for sk, val in need.items():
            if sk == "pe" and eng == "pe":
                continue
            if kn.get(sk, 0) < val:
                kn[sk] = val
                waits.append((sk, val))
        return waits

    def _record(self, reads, writes, sk, val):
        for w in writes:
            lst = self.recs.setdefault(w[0], [])
            lst[:] = [x for x in lst if not (x[0][1] >= w[1] and x[0][2] <= w[2] and x[0][3] >= w[3] and x[0][4] <= w[4])]
            lst.append((w, sk, val, True))
        for r in reads:
            lst = self.recs.setdefault(r[0], [])
            lst[:] = [x for x in lst if not ((not x[3]) and x[1] == sk and x[0] == r)]
            lst.append((r, sk, val, False))

    def _split(self, kwargs, extra_reads, extra_writes):
        reads, writes = [], []
        for k, v in kwargs.items():
            if isinstance(v, bass.AP):
                if k in _WRITE_KW:
                    writes += _region(v)
                else:
                    reads += _region(v)
        for a in extra_reads:
            reads += _region(a)
        for a in extra_writes:
            writes += _region(a)
        return reads, writes

    def I(self, eng, meth, *args, reads=(), writes=(), **kwargs):
        r, w = self._split(kwargs, reads, writes)
        waits = self._deps(eng, r, w)
        self.cnt[eng] += 1
        val = self.cnt[eng]
        self._record(r, w, eng, val)
        self.stream[eng].append(("op", waits, meth, args, kwargs, eng, 1))
        self.n_instr += 1

    def dma(self, q, out, in_, meth="dma_start", extra_reads=(), **kw):
        r = list(_region(in_))
        w = list(_region(out))
        for a in extra_reads:
            r += _region(a)
        for k, v in kw.items():
            if isinstance(v, bass.AP):
                r += _region(v)
        waits = self._deps(q, r, w)
        if q == "pool":
            i = self.pd_rr
            self.pd_rr = (self.pd_rr + 1) % self.pd_n
            sk = ("pd", i)
            tgt = self.pd_tgt
        else:
            i = self.dma_rr
            self.dma_rr = (self.dma_rr + 1) % self.n_dma_sems
            sk = ("d", i)
            tgt = self.dma_tgt
        prev = tgt[i]
        if prev > 0 and self.known[q].get(sk, 0) < prev:
            self.known[q][sk] = prev
            waits.append((sk, prev))
        tgt[i] = prev + 16
        val = prev + 16
        self._record(r, w, sk, val)
        kwargs = dict(out=out, in_=in_)
        kwargs.update(kw)
        self.stream[q].append(("op", waits, meth, (), kwargs, sk, 16))
        self.n_instr += 1

    def emit(self, final_engine="sp"):
        nc = self.nc
        import contextlib
        with contextlib.ExitStack() as st:
            for e in ("pe", "dve", "act", "pool"):
                self.sems[e] = st.enter_context(nc.semaphore("s_" + e))
            for i in range(self.n_dma_sems):
                self.sems[("d", i)] = st.enter_context(nc.semaphore("s_d%d" % i))
            fin = []
            for j in range(self.pd_n):
                self.sems[("pd", j)] = st.enter_context(nc.semaphore("s_pd%d" % j))
                if self.pd_tgt[j] > 0:
                    fin.append((("pd", j), self.pd_tgt[j]))
            for i in range(self.n_dma_sems):
                if self.dma_tgt[i] > 0:
                    fin.append((("d", i), self.dma_tgt[i]))
            for e in ("pe", "dve", "act", "pool"):
                if self.cnt[e] > 0:
                    fin.append((e, self.cnt[e]))
            block = st.enter_context(nc.Block())
            handles = dict(pe="tensor", dve="vector", act="scalar", pool="gpsimd", sp="sync")

            def run(engname, eobj):
                for item in self.stream[engname]:
                    if item[0] == "clear":
                        eobj.sem_clear(self.sems[item[1]])
                        continue
                    (_, waits, meth, args, kwargs, sk, inc) = item
                    for (wk, wv) in waits:
                        eobj.wait_ge(self.sems[wk], wv)
                    ins = getattr(eobj, meth)(*args, **kwargs)
                    ins.then_inc(self.sems[sk], inc)
                if engname == final_engine:
                    for (wk, wv) in fin:
                        eobj.wait_ge(self.sems[wk], wv)

            @block.sync
            def _(e):
                run("sp", e)

            @block.tensor
            def _(e):
                run("pe", e)

            @block.vector
            def _(e):
                run("dve", e)

            @block.scalar
            def _(e):
                run("act", e)

            @block.gpsimd
            def _(e):
                run("pool", e)


import contextlib

P = 128
D = 1024
DC = 8
TP = 2048
NS = 16
NT = TP + NS
FH = 2816
FC = 22
TT = [(0, 512), (512, 512), (1024, 512), (1536, 512), (2048, 16)]
HALVES = [[(0, 512), (512, 512)], [(1024, 512), (1536, 512), (2048, 16)]]
RMS_EPS = 1e-6
LN_EPS = 1e-5
GELU = AF.Gelu_apprx_tanh

N_LAYERS = 4


class K:
    pass


WITH_CACHE = True
import os
DBG = int(os.environ.get('KDBG', '0'))
FARFIX = int(os.environ.get('KFAR', '1'))


def build_program(n_layers=N_LAYERS):
    nc = bass.Bass("TRN2", target_bir_lowering=False)
    S = Sched(nc)
    k = K()
    k.nc, k.S = nc, S
    k.with_cache = WITH_CACHE

    def din(name, shape, dt=F32):
        return nc.dram_tensor(name, list(shape), dt, kind="ExternalInput").ap()

    def dout(name, shape, dt=F32):
        return nc.dram_tensor(name, list(shape), dt, kind="ExternalOutput").ap()

    I = {}
    I["xp"] = din("xp", [TP, D])
    I["xs"] = din("xs", [NS, D])
    I["st_conv"] = din("st_conv", [4, NS, 2 * FH])
    I["norm_mix_g"] = din("norm_mix_g", [4, D])
    I["norm_ffn_g"] = din("norm_ffn_g", [4, D])
    I["norm_final_g"] = din("norm_final_g", [1, D])
    I["gm_w_in"] = din("gm_w_in", [D, 2 * D])
    I["gm_ln_g"] = din("gm_ln_g", [1, D])
    I["gm_ln_b"] = din("gm_ln_b", [1, D])
    I["gm_w_s"] = din("gm_w_s", [8, 128, 128])
    I["gm_b_s"] = din("gm_b_s", [8, 128])
    I["gm_w_out"] = din("gm_w_out", [D, D])
    I["ffn_w_in"] = din("ffn_w_in", [4, D, 2 * FH])
    I["ffn_conv_w"] = din("ffn_conv_w", [4, 3, FH])
    I["ffn_conv_b"] = din("ffn_conv_b", [4, FH])
    I["ffn_w_out"] = din("ffn_w_out", [4, FH, D])
    I["moba_w_qkv"] = din("moba_w_qkv", [D, 3 * D])
    I["moba_w_o"] = din("moba_w_o", [D, D])
    I["rel_bias"] = din("rel_bias", [32, 16])
    I["t5_oh"] = din("t5_oh", [33, 512])
    I["t5_oh15"] = din("t5_oh15", [32, 128])
    I["st_pool"] = din("st_pool", [NS, 15, D])
    I["pool_w"] = din("pool_w", [4, 256, 256])
    I["pool_scale"] = din("pool_scale", [1, D])
    I["st_shift"] = din("st_shift", [NS, D])
    I["st_wkv"] = din("st_wkv", [NS, 16, 64, 64])
    I["rw_vecs"] = din("rw_vecs", [104, P])
    for nm in ("rwkv_w_r", "rwkv_w_k", "rwkv_w_v", "rwkv_w_o"):
        I[nm] = din(nm, [D, D])
    I["rwkv_w1"] = din("rwkv_w1", [D, 64])
    I["rwkv_a1"] = din("rwkv_a1", [D, 64])
    I["rwkv_g1"] = din("rwkv_g1", [D, 128])
    I["rwkv_w2"] = din("rwkv_w2", [64, D])
    I["rwkv_a2"] = din("rwkv_a2", [64, D])
    I["rwkv_g2"] = din("rwkv_g2", [128, D])
    I["rwkv_lnx"] = din("rwkv_lnx", [2, D])
    I["page_table"] = din("page_table", [1, NS * 16], I32)
    if WITH_CACHE:
        I["cache_k"] = din("cache_k", [2560 * 128, D])
        I["cache_v"] = din("cache_v", [2560 * 128, D])
    O = {}
    O["mk_p"] = dout("mk_p", [TP, D])
    O["mv_p"] = dout("mv_p", [TP, D])
    O["mk_s"] = dout("mk_s", [NS, D])
    O["mv_s"] = dout("mv_s", [NS, D])
    O["pool_p"] = dout("pool_p", [15, D])
    O["pool_s"] = dout("pool_s", [NS, 15, D])
    O["shift_p"] = dout("shift_p", [1, D])
    O["shift_s"] = dout("shift_s", [NS, D])
    O["wkv_p"] = dout("wkv_p", [16, 64, 64])
    O["wkv_s"] = dout("wkv_s", [NS, 16, 64, 64])
    O["y_p"] = dout("y_p", [TP, D])
    O["y_s"] = dout("y_s", [NS, D])
    O["gm_v"] = dout("gm_v", [NS, D])
    O["conv_p"] = dout("conv_p", [4, 2, FH])
    O["conv_s"] = dout("conv_s", [4, NS, 2 * FH])
    k.I, k.O = I, O

    with contextlib.ExitStack() as st:
        def sb(name, shape, dt=F32):
            return st.enter_context(nc.sbuf_tensor(name, list(shape), dt))

        k.sbt = sb
        k.XT = sb("XT", [P, DC, NT], F32)
        k.HT = sb("HT", [P, DC, NT], BF16)
        k.HID = sb("HID", [P, FC * 1040], BF16)
        k.WA = sb("WA", [P, 16384], BF16)
        k.SCRF = sb("SCRF", [P, 4096], F32)
        k.RSTD = sb("RSTD", [P, 512], F32)
        k.TOK = sb("TOK", [P, 1024], F32)
        k.TOK2 = sb("TOK2", [P, 1024], F32)
        k.ones_bf = sb("ones_bf", [P, P], BF16)
        k.ident_f = sb("ident_f", [P, P], F32)
        k.gmix = sb("gmix", [P, 4, DC], F32)
        k.gffn = sb("gffn", [P, 4, DC], F32)
        k.gfin = sb("gfin", [P, DC], F32)
        k.cw = sb("cw", [P, 4, 3, FC], F32)
        k.cb = sb("cb", [P, 4, FC], F32)
        k.small = sb("small", [P, 64], F32)
        k.stat = sb("stat", [P, 32], F32)
        k.SM2 = sb("SM2", [P, 256], F32)
        k.ones_f = sb("ones_f", [P, P], F32)
        k.OSMP = sb("OSMP", [P, DC, NS], BF16)
        k.IDX = sb("IDX", [P, NS * 16], I32)
        k.GT = sb("GT", [P, FC, 2], F32)
        k.psum = [st.enter_context(nc.psum_tensor("ps%d" % i, [P, 512], F32)) for i in range(6)]
        k.psumb = [st.enter_context(nc.psum_tensor("psb%d" % i, [P, 1024], BF16)) for i in range(2)]
        k.ps_rr = 0
        k.psb_rr = 0

        def ps():
            b = k.psum[k.ps_rr]
            k.ps_rr = (k.ps_rr + 1) % 6
            return b

        def psb():
            b = k.psumb[k.psb_rr]
            k.psb_rr = (k.psb_rr + 1) % 2
            return b

        k.ps = ps
        k.psb = psb
        k.ident_b = sb("ident_b", [P, P], BF16)
        k.tz = nc.dram_tensor("tz", [16, 128, 512], F32, kind="Internal")
        k.SQ = k.HID[:, 0:4096].rearrange("p (c n) -> p c n", c=DC)

        setup_consts(k)
        load_x(k)
        if n_layers >= 1:
            layer0_gmlp(k)
            ffn(k, 0)
        if n_layers >= 2:
            layer1_moba(k)
        if n_layers >= 3:
            layer2_pool(k)
        if n_layers >= 4:
            layer3_rwkv(k)
        final_norm(k)
        S.emit()
    _PROG['cnt'] = dict(S.cnt); _PROG['n'] = S.n_instr
    return nc


def setup_consts(k):
    S, nc = k.S, k.nc
    S.I("pool", "memset", ap=k.ones_bf[:], constant=1.0)
    S.I("pool", "memset", ap=k.ones_f[:], constant=1.0)
    S.I("pool", "memset", ap=k.ident_f[:], constant=1.0)
    S.I("pool", "affine_select", out=k.ident_f[:], in_=k.ident_f[:], pattern=[[1, P]],
        compare_op=ALU.is_equal, fill=0.0, base=0, channel_multiplier=-1)
    S.I("dve", "tensor_copy", out=k.ident_b[:], in_=k.ident_f[:])
    S.I("pool", "memset", ap=k.small[:, 0:1], constant=RMS_EPS)
    S.I("pool", "memset", ap=k.small[:, 1:2], constant=LN_EPS)
    S.I("pool", "memset", ap=k.small[:, 2:3], constant=64e-5)
    SL = dict(allow_slow_non_contiguous=True)
    S.dma("sp", k.gmix[:], k.I["norm_mix_g"].rearrange("l (c p) -> p l c", p=P), **SL)
    S.dma("sp", k.gffn[:], k.I["norm_ffn_g"].rearrange("l (c p) -> p l c", p=P), **SL)
    S.dma("sp", k.gfin[:], k.I["norm_final_g"].rearrange("l (c p) -> p (l c)", p=P), **SL)
    for l in range(4):
        S.dma("sp", k.cw[:, l], k.I["ffn_conv_w"][l].rearrange("j (c p) -> p j c", p=P), **SL)
    S.dma("sp", k.cb[:], k.I["ffn_conv_b"].rearrange("l (c p) -> p l c", p=P), **SL)


def load_x(k):
    S = k.S
    for tt in range(17):
        n = P if tt < 16 else NS
        src = k.I["xp"][tt * P:(tt + 1) * P, :] if tt < 16 else k.I["xs"]
        buf = k.TOK if tt % 2 == 0 else k.TOK2
        S.dma("sp", buf[0:n, :], src)
        for half in range(2):
            pt = k.ps()
            for j in range(4):
                c = half * 4 + j
                S.I("pe", "transpose", out=pt[:, j * n:(j + 1) * n], in_=buf[0:n, c * P:(c + 1) * P],
                    identity=k.ident_f[0:n, 0:n])
            dst = k.XT[:, half * 4:half * 4 + 4, tt * P:tt * P + n]
            srcp = pt[:, 0:4 * n].rearrange("p (c n) -> p c n", c=4)
            if half == 0:
                S.I("act", "activation", out=dst, in_=srcp, func=AF.Copy)
            else:
                S.I("dve", "tensor_copy", out=dst, in_=srcp)


def rmsnorm(k, gcols, out_bf=None, out_f32=None):
    S = k.S
    for (n0, n) in TT:
        S.I("act", "activation", out=k.SQ[:, :, 0:n], in_=k.XT[:, :, n0:n0 + n], func=AF.Square)
        pt = k.ps()
        for c in range(DC):
            S.I("pe", "matmul", out=pt[:, 0:n], lhsT=k.ones_bf[:], rhs=k.SQ[:, c, 0:n],
                start=(c == 0), stop=(c == DC - 1))
        S.I("act", "activation", out=k.RSTD[:, 0:n], in_=pt[:, 0:n], func=AF.Sqrt,
            scale=1.0 / D, bias=k.small[:, 0:1])
        S.I("dve", "reciprocal", out=k.RSTD[:, 0:n], in_=k.RSTD[:, 0:n])
        for c in range(DC):
            for o in (out_bf, out_f32):
                if o is None:
                    continue
                S.I("dve", "scalar_tensor_tensor", out=o[:, c, n0:n0 + n], in0=k.XT[:, c, n0:n0 + n],
                    scalar=gcols[:, c:c + 1], in1=k.RSTD[:, 0:n], op0=ALU.mult, op1=ALU.mult)


def layer0_gmlp(k):
    S, nc = k.S, k.nc
    I, O = k.I, k.O
    rmsnorm(k, k.gmix[:, 0, :], out_bf=k.HT)
    Wv = k.WA[:, 0:8192].rearrange("p (c n) -> p c n", c=DC)
    Wu = k.WA[:, 8192:16384].rearrange("p (c n) -> p c n", c=DC)
    w_in = I["gm_w_in"].rearrange("(c p) n -> p c n", p=P)
    S.dma("pool", Wv, w_in[:, :, D:2 * D])
    S.dma("pool", Wu, w_in[:, :, 0:D])
    US = k.HID[:, 0:DC * NT].rearrange("p (c n) -> p c n", c=DC)
    VB = k.HID[:, 16512:16512 + 4096].rearrange("p (c n) -> p c n", c=4)
    WcT = k.HID[:, 20608:20608 + 1024].rearrange("p (g t) -> p g t", g=8)
    lng = k.SCRF[:, 0:1024]
    lnb = k.SCRF[:, 1024:2048]
    BSB = k.SCRF[:, 2048:3072].rearrange("p (g t) -> p g t", g=8)
    GU = k.SCRF[:, 3072:3584]
    STMP = k.SCRF[:, 3584:4096]
    w00 = k.small[:, 8:16]
    b00 = k.small[:, 16:24]
    S.dma("sp", lng, I["gm_ln_g"].broadcast_to([P, D]))
    S.dma("sp", lnb, I["gm_ln_b"].broadcast_to([P, D]))
    bs_flat = I["gm_b_s"].rearrange("g t -> (g t)")
    S.dma("sp", k.SCRF[:, 2048:3072], bs_flat.partition_broadcast(P))
    SL = dict(allow_slow_non_contiguous=True)
    S.dma("sp", w00, I["gm_w_s"][:, 0, 0:1].rearrange("g o -> (g o)").partition_broadcast(P), **SL)
    S.dma("sp", b00, I["gm_b_s"][:, 0:1].rearrange("g o -> (g o)").partition_broadcast(P), **SL)
    for g in range(8):
        S.dma("sp", k.TOK[:, g * P:(g + 1) * P], I["gm_w_s"][g])
    for half in range(2):
        pt = k.ps()
        for j in range(4):
            g = half * 4 + j
            S.I("pe", "transpose", out=pt[:, j * P:(j + 1) * P], in_=k.TOK[:, g * P:(g + 1) * P], identity=k.ident_f[:])
        S.I("act", "activation", out=k.TOK2[:, half * 512:(half + 1) * 512], in_=pt[:], func=AF.Copy)
    for g in range(8):
        S.I("pool", "affine_select", out=WcT[:, g, :], in_=k.TOK2[:, g * P:(g + 1) * P], pattern=[[1, P]],
            compare_op=ALU.is_ge, fill=0.0, base=0, channel_multiplier=-1)

    def v_phase(t0, n, dst_bf, dst_f32=None, buf=None):
        pa, pb = k.ps(), k.ps()
        for hh, pt in enumerate((pa, pb)):
            for c in range(DC):
                S.I("pe", "matmul", out=pt[0:n, :], lhsT=k.HT[:, c, t0:t0 + n], rhs=Wv[:, c, hh * 512:(hh + 1) * 512],
                    start=(c == 0), stop=(c == DC - 1))
            S.I("act", "activation", out=buf[0:n, hh * 512:(hh + 1) * 512], in_=pt[0:n, :], func=GELU)
        st6 = k.stat[:, 0:12]
        for hh in range(2):
            S.I("dve", "bn_stats", out=k.stat[0:n, hh * 6:(hh + 1) * 6], in_=buf[0:n, hh * 512:(hh + 1) * 512])
        S.I("dve", "bn_aggr", out=k.stat[0:n, 12:14], in_=k.stat[0:n, 0:12])
        S.I("act", "activation", out=k.stat[0:n, 14:15], in_=k.stat[0:n, 13:14], func=AF.Sqrt, bias=k.small[0:n, 1:2], scale=1.0)
        S.I("dve", "reciprocal", out=k.stat[0:n, 15:16], in_=k.stat[0:n, 14:15])
        S.I("dve", "tensor_scalar", out=buf[0:n, :], in0=buf[0:n, :], scalar1=k.stat[0:n, 12:13], scalar2=k.stat[0:n, 15:16],
            op0=ALU.subtract, op1=ALU.mult)
        S.I("dve", "tensor_tensor", out=buf[0:n, :], in0=buf[0:n, :], in1=lng[0:n, :], op=ALU.mult)
        if dst_f32 is not None:
            S.I("dve", "tensor_tensor", out=dst_f32, in0=buf[0:n, :], in1=lnb[0:n, :], op=ALU.add)
        if dst_bf is not None:
            S.I("dve", "tensor_tensor", out=dst_bf, in0=buf[0:n, :], in1=lnb[0:n, :], op=ALU.add)

    def u_us(g, n0, n, s_ap):
        pu = k.ps()
        for c in range(DC):
            S.I("pe", "matmul", out=pu[:, 0:n], lhsT=Wu[:, c, g * P:(g + 1) * P], rhs=k.HT[:, c, n0:n0 + n],
                start=(c == 0), stop=(c == DC - 1))
        S.I("act", "activation", out=GU[:, 0:n], in_=pu[:, 0:n], func=GELU)
        S.I("dve", "tensor_tensor", out=US[:, g, n0:n0 + n], in0=GU[:, 0:n], in1=s_ap, op=ALU.mult)

    for ti in range(4):
        n0 = ti * 512
        for j in range(4):
            v_phase(n0 + j * P, P, VB[:, j, :], buf=(k.TOK if j % 2 == 0 else k.TOK2))
        for g in range(8):
            pss = k.ps()
            for j in range(4):
                S.I("pe", "matmul", out=pss[:, j * P:(j + 1) * P], lhsT=VB[:, j, g * P:(g + 1) * P], rhs=WcT[:, g, :],
                    start=True, stop=True)
            S.I("dve", "tensor_tensor", out=STMP.rearrange("p (j t) -> p j t", j=4),
                in0=pss[:].rearrange("p (j t) -> p j t", j=4),
                in1=BSB[:, g:g + 1, :].broadcast_to([P, 4, P]), op=ALU.add)
            u_us(g, n0, 512, STMP)
    v_phase(TP, NS, None, dst_f32=k.TOK2[0:NS, :], buf=k.TOK)
    S.dma("sp", O["gm_v"], k.TOK2[0:NS, :])
    pvt = k.ps()
    for g in range(8):
        S.I("pe", "transpose", out=pvt[:, g * NS:(g + 1) * NS], in_=k.TOK2[0:NS, g * P:(g + 1) * P], identity=k.ident_f[0:NS, 0:NS])
    for g in range(8):
        S.I("dve", "tensor_scalar", out=STMP[:, g * NS:(g + 1) * NS], in0=pvt[:, g * NS:(g + 1) * NS],
            scalar1=w00[:, g:g + 1], scalar2=b00[:, g:g + 1], op0=ALU.mult, op1=ALU.add)
    for g in range(8):
        u_us(g, TP, NS, STMP[:, g * NS:(g + 1) * NS])
    Wo = k.WA[:, 0:8192].rearrange("p (c n) -> p c n", c=DC)
    S.dma("pool", Wo, I["gm_w_out"].rearrange("(c p) n -> p c n", p=P))
    for (n0, n) in TT:
        for m in range(DC):
            po = k.ps()
            for c in range(DC):
                S.I("pe", "matmul", out=po[:, 0:n], lhsT=Wo[:, c, m * P:(m + 1) * P], rhs=US[:, c, n0:n0 + n],
                    start=(c == 0), stop=(c == DC - 1))
            S.I("dve", "tensor_tensor", out=k.XT[:, m, n0:n0 + n], in0=po[:, 0:n], in1=k.XT[:, m, n0:n0 + n], op=ALU.add)


def layer1_moba(k):
    S, nc = k.S, k.nc
    I, O = k.I, k.O
    rmsnorm(k, k.gmix[:, 1, :], out_bf=k.HT)
    wqkv = I["moba_w_qkv"].rearrange("(c p) n -> p c n", p=P)
    Wk = k.WA[:, 0:8192].rearrange("p (c n) -> p c n", c=DC)
    Wv = k.WA[:, 8192:16384].rearrange("p (c n) -> p c n", c=DC)
    S.dma("pool", Wk, wqkv[:, :, D:2 * D])
    S.dma("pool", Wv, wqkv[:, :, 2 * D:3 * D])
    QS = k.SCRF[0:NS, 0:1024]
    KS = k.SCRF[0:NS, 1024:2048]
    VS = k.SCRF[0:NS, 2048:3072]
    for tt in range(17):
        n = P if tt < 16 else NS
        t0 = tt * P
        for wi, (W, op, os_) in enumerate(((Wk, O["mk_p"], O["mk_s"]), (Wv, O["mv_p"], O["mv_s"]))):
            buf = k.TOK if wi == 0 else k.TOK2
            if tt == 16:
                buf = KS if wi == 0 else VS
            for hh in range(2):
                pt = k.ps()
                for c in range(DC):
                    S.I("pe", "matmul", out=pt[0:n, :], lhsT=k.HT[:, c, t0:t0 + n], rhs=W[:, c, hh * 512:(hh + 1) * 512],
                        start=(c == 0), stop=(c == DC - 1))
                if hh == 0:
                    S.I("act", "activation", out=buf[0:n, 0:512], in_=pt[0:n, :], func=AF.Copy)
                else:
                    S.I("dve", "tensor_copy", out=buf[0:n, 512:1024], in_=pt[0:n, :])
            if tt < 16:
                S.dma("sp", op[t0:t0 + n, :], buf[0:n, :])
            else:
                S.dma("sp", os_, buf[0:n, :])
    Wq = k.WA[:, 0:8192].rearrange("p (c n) -> p c n", c=DC)
    S.dma("pool", Wq, wqkv[:, :, 0:D])
    for hh in range(2):
        pt = k.ps()
        for c in range(DC):
            S.I("pe", "matmul", out=pt[0:NS, :], lhsT=k.HT[:, c, TP:NT], rhs=Wq[:, c, hh * 512:(hh + 1) * 512],
                start=(c == 0), stop=(c == DC - 1))
        S.I("act", "activation", out=QS[:, hh * 512:(hh + 1) * 512], in_=pt[0:NS, :], func=AF.Copy)
    OT = k.HID[:, 0:DC * NT].rearrange("p (c n) -> p c n", c=DC)
    if k.with_cache:
        moba_sample_attention(k, OT)
    else:
        S.I("pool", "memset", ap=k.OSMP[:], constant=0.0)
    moba_prompt_attention(k, OT)
    S.I("dve", "tensor_copy", out=OT[:, :, TP:NT], in_=k.OSMP[:])
    moba_out_proj(k, OT)
    ffn(k, 1)


NEG = -30000.0


def moba_bias_tiles(k):
    S, nc, I = k.S, k.nc, k.I
    RB = k.TOK2[0:33, 0:16]
    OH = k.TOK2[0:33, 512:1024]
    S.I("pool", "memset", ap=k.TOK2[32:33, 0:16], constant=NEG)
    S.dma("sp", k.TOK2[0:32, 0:16], I["rel_bias"])
    S.dma("sp", OH, I["t5_oh"])
    pt = k.ps()
    S.I("pe", "matmul", out=pt[0:16, :], lhsT=RB, rhs=OH, start=True, stop=True)
    S.I("act", "activation", out=k.TOK[0:16, 0:512], in_=pt[0:16, :], func=AF.Copy)
    src = bass.AP(tensor=k.TOK.tensor if hasattr(k.TOK, "tensor") else k.TOK, offset=0, ap=[[1024, 16], [0, 128], [1, 512]])
    S.dma("sp", k.tz.ap(), src)
    BT = k.SCRF[:, 0:4096].rearrange("p (h d j) -> p h d j", h=16, d=2)
    for d in range(2):
        a = bass.AP(tensor=k.tz, offset=127 + 256 * d, ap=[[511, 128], [65536, 16], [1, 128]])
        S.dma("sp", BT[:, :, d, :], a)
    S.dma("sp", k.small[:, 32:48], I["rel_bias"][31:32, :].broadcast_to([P, 16]))
    return BT


def moba_prompt_attention(k, OT):
    S, nc, I = k.S, k.nc, k.I
    BT = moba_bias_tiles(k)
    if DBG == 1:
        return
    rb31 = k.small[:, 32:48]
    wqkv = I["moba_w_qkv"].rearrange("(c p) n -> p c n", p=P)
    WA = k.WA
    QT = WA[:, 6144:8192]
    KT = WA[:, 8192:10240]
    VC = WA[:, 10240:10240 + 2080].rearrange("p (t h e) -> p t h e", t=16, h=2)
    kmT = WA[:, 12320:12328]
    OQ = WA[:, 12336:12336 + 128]
    PB = [k.HID[:, 16512:16512 + 2048], k.HID[:, 18560:18560 + 2048]]
    PT = [k.HID[:, 20608:20608 + 512], k.HID[:, 21120:21120 + 512]]
    GA = k.TOK2[:, 0:256].rearrange("p (q h n) -> p q h n", q=16, h=2)
    MA = k.TOK2[:, 256:512].rearrange("p (q h n) -> p q h n", q=16, h=2)
    NEAR = k.TOK[:, 0:256]
    S.I("pool", "memset", ap=WA[:, 10240:10240 + 2080], constant=1.0)
    unit = 0
    for c in range(8):
        par = c % 2
        wq = WA[:, par * 3072:par * 3072 + 1024].rearrange("p (c n) -> p c n", c=DC)
        wk = WA[:, par * 3072 + 1024:par * 3072 + 2048].rearrange("p (c n) -> p c n", c=DC)
        wv = WA[:, par * 3072 + 2048:par * 3072 + 3072].rearrange("p (c n) -> p c n", c=DC)
        S.dma("pool", wq, wqkv[:, :, c * P:(c + 1) * P])
        S.dma("pool", wk, wqkv[:, :, D + c * P:D + (c + 1) * P])
        S.dma("pool", wv, wqkv[:, :, 2 * D + c * P:2 * D + (c + 1) * P])
        for ti in range(4):
            for (w, dst) in ((wq, QT), (wk, KT)):
                pt = k.ps()
                for dc in range(DC):
                    S.I("pe", "matmul", out=pt[:], lhsT=w[:, dc, :], rhs=k.HT[:, dc, ti * 512:(ti + 1) * 512],
                        start=(dc == 0), stop=(dc == DC - 1))
                S.I("act", "activation", out=dst[:, ti * 512:(ti + 1) * 512], in_=pt[:], func=AF.Copy)
        for tg in range(4):
            pt = k.ps()
            for j in range(4):
                tt = tg * 4 + j
                for dc in range(DC):
                    S.I("pe", "matmul", out=pt[:, j * P:(j + 1) * P], lhsT=k.HT[:, dc, tt * P:(tt + 1) * P], rhs=wv[:, dc, :],
                        start=(dc == 0), stop=(dc == DC - 1))
            S.I("dve", "tensor_copy", out=VC[:, tg * 4:(tg + 1) * 4, :, 0:64],
                in_=pt[:].rearrange("p (t h e) -> p t h e", t=4, h=2))
        S.I("dve", "tensor_reduce", out=k.stat[:, 16:24], in_=KT.rearrange("p (n s) -> p n s", n=8), axis=AX.X, op=ALU.add)
        S.I("dve", "tensor_scalar", out=kmT, in0=k.stat[:, 16:24], scalar1=1.0 / 256, scalar2=None, op0=ALU.mult)
        pg = k.ps()
        for qt in range(16):
            for h2 in range(2):
                S.I("pe", "matmul", out=pg[:, (qt * 2 + h2) * 8:(qt * 2 + h2 + 1) * 8],
                    lhsT=QT[h2 * 64:(h2 + 1) * 64, qt * P:(qt + 1) * P], rhs=kmT[h2 * 64:(h2 + 1) * 64, 0:8], start=True, stop=True)
        S.I("pool", "memset", ap=k.TOK2[:, 0:256], constant=-1e30)
        S.I("pool", "memset", ap=k.TOK2[:, 256:512], constant=0.0)
        for qt in range(8, 16):
            own = qt // 2
            S.I("dve", "tensor_copy", out=GA[:, qt, :, 0:own],
                in_=pg[:, qt * 16:(qt + 1) * 16].rearrange("p (h n) -> p h n", h=2)[:, :, 0:own])
            for h2 in range(2):
                S.I("dve", "max", out=k.stat[:, 24:32], in_=GA[:, qt, h2, :])
                S.I("dve", "tensor_scalar", out=MA[:, qt, h2, 0:own], in0=GA[:, qt, h2, 0:own], scalar1=k.stat[:, 26:27], scalar2=NEG,
                    op0=ALU.is_lt, op1=ALU.mult)
        if DBG == 2:
            continue
        for qt in range(int(os.environ.get("KQT", "16"))):
            own = qt // 2
            nk = qt + 1
            ngrp = (nk + 3) // 4
            for h2 in range(2):
                h = 2 * c + h2
                Pb = PB[unit % 2]
                unit += 1
                qs = QT[h2 * 64:(h2 + 1) * 64, qt * P:(qt + 1) * P]
                banks = []
                for kg in range(ngrp):
                    w = min(512, nk * P - kg * 512)
                    pl = k.ps()
                    S.I("pe", "matmul", out=pl[:, 0:w], lhsT=qs, rhs=KT[h2 * 64:(h2 + 1) * 64, kg * 512:kg * 512 + w], start=True, stop=True)
                    banks.append((pl, w))
                for kg, (pl, w) in enumerate(banks):
                    S.I("dve", "tensor_reduce", out=k.stat[:, kg:kg + 1], in_=pl[:, 0:w], axis=AX.X, op=ALU.max)
                S.I("dve", "tensor_reduce", out=k.stat[:, 4:5], in_=k.stat[:, 0:ngrp], axis=AX.X, op=ALU.max)
                negb = k.stat[:, 5:6]
                S.I("dve", "tensor_scalar", out=negb, in0=k.stat[:, 4:5], scalar1=-0.125, scalar2=None, op0=ALU.mult)
                bc = k.stat[:, 6:14]
                S.I("dve", "tensor_scalar", out=bc, in0=MA[:, qt, h2, :], scalar1=negb, scalar2=rb31[:, h:h + 1], op0=ALU.add, op1=ALU.add)
                kt = 0
                while kt <= qt - 2:
                    n = kt // 2
                    ntile = 2 if (kt % 2 == 0 and kt + 1 <= qt - 2) else 1
                    pl, _ = banks[kt // 4]
                    o = (kt % 4) * P
                    if FARFIX:
                        fr = k.TOK[:, 256:256 + ntile * P]
                        S.I("dve", "tensor_scalar", out=fr, in0=pl[:, o:o + ntile * P], scalar1=0.125, scalar2=None, op0=ALU.mult)
                        S.I("act", "activation", out=Pb[:, kt * P:(kt + ntile) * P], in_=fr, func=AF.Exp, scale=1.0, bias=bc[:, n:n + 1])
                    else:
                        S.I("act", "activation", out=Pb[:, kt * P:(kt + ntile) * P], in_=pl[:, o:o + ntile * P], func=AF.Exp,
                            scale=0.125, bias=bc[:, n:n + 1])
                    kt += ntile
                for kt in (qt - 1, qt):
                    if kt < 0:
                        continue
                    d = qt - kt
                    pl, _ = banks[kt // 4]
                    o = (kt % 4) * P
                    nr = NEAR[:, d * P:(d + 1) * P]
                    S.I("dve", "scalar_tensor_tensor", out=nr, in0=pl[:, o:o + P], scalar=0.125, in1=BT[:, h, d, :], op0=ALU.mult, op1=ALU.add)
                    if d == 1 and qt % 2 == 0:
                        bcol = k.stat[:, 14:15]
                        S.I("dve", "tensor_scalar", out=bcol, in0=MA[:, qt, h2, own - 1:own], scalar1=negb, scalar2=None, op0=ALU.add)
                    else:
                        bcol = negb
                    S.I("act", "activation", out=Pb[:, kt * P:(kt + 1) * P], in_=nr, func=AF.Exp, scale=1.0, bias=bcol)
                if DBG == 4:
                    continue
                acc = k.ps()
                for kg in range(ngrp):
                    nt = min(4, nk - kg * 4)
                    pb_ = k.psb()
                    for j in range(nt):
                        kt = kg * 4 + j
                        S.I("pe", "transpose", out=pb_[:, j * P:(j + 1) * P], in_=Pb[:, kt * P:(kt + 1) * P], identity=k.ident_b[:])
                    ptb = PT[kg % 2]
                    S.I("dve", "tensor_copy", out=ptb[:, 0:nt * P], in_=pb_[:, 0:nt * P])
                    if DBG == 5:
                        continue
                    for j in range(nt):
                        kt = kg * 4 + j
                        S.I("pe", "matmul", out=acc[:, 0:65], lhsT=ptb[:, j * P:(j + 1) * P], rhs=VC[:, kt, h2, :],
                            start=(kt == 0), stop=(kt == qt))
                if DBG in (5, 6):
                    continue
                ACS = k.SM2[:, 0:65]
                S.I("act", "activation", out=ACS, in_=acc[:, 0:65], func=AF.Copy)
                S.I("dve", "reciprocal", out=k.stat[:, 15:16], in_=ACS[:, 64:65])
                S.I("dve", "tensor_scalar", out=OQ[:, h2 * 64:(h2 + 1) * 64], in0=ACS[:, 0:64], scalar1=k.stat[:, 15:16], scalar2=None, op0=ALU.mult)
            if DBG in (4, 5, 6):
                continue
            pb_ = k.psb()
            S.I("pe", "transpose", out=pb_[:, 0:P], in_=OQ, identity=k.ident_b[:])
            S.I("dve", "tensor_copy", out=OT[:, c, qt * P:(qt + 1) * P], in_=pb_[:, 0:P])


def moba_sample_attention(k, OT):
    S, nc, I = k.S, k.nc, k.I
    QS = k.SCRF[0:NS, 0:1024]
    KS = k.SCRF[0:NS, 1024:2048]
    VS = k.SCRF[0:NS, 2048:3072]
    WA = k.WA
    KSL = [WA[:, 0:4096].rearrange("p (j n) -> p j n", j=4), WA[:, 4096:8192].rearrange("p (j n) -> p j n", j=4)]
    VPG = [WA[:, 8192 + i * 1024:8192 + (i + 1) * 1024] for i in range(4)]
    H = k.HID
    QB = H[:, 0:1024]
    EB = H[:, 1024:1280].rearrange("p (j h) -> p j h", j=16)
    ES = H[0:NS, 1280:1296]
    QSb = H[0:NS, 2048:3072]
    VSb = H[0:NS, 3072:4096]
    MK = H[0:16, 4096:5120]
    SELb = H[0:NS, 5120:5120 + 2048].rearrange("p (s m) -> p s m", s=NS)
    ASEL = H[0:16, 7168:7168 + 256].rearrange("p (s m) -> p s m", s=NS)
    DMb = H[0:16, 7424:7424 + 1024].rearrange("p (a d) -> p a d", a=16)
    L = k.TOK2[:, 0:256].rearrange("p (j h) -> p j h", j=16)
    CADD = k.TOK2[:, 256:512].rearrange("p (j h) -> p j h", j=16)
    COL = k.TOK2[:, 512:768]
    GS = k.TOK2[:, 768:896].rearrange("p (h n) -> p h n", h=16)
    MS = k.TOK2[:, 896:1024].rearrange("p (h n) -> p h n", h=16)
    sm = k.SM2
    rb31b, rb0b, BS15, DB15 = sm[:, 0:16], sm[:, 16:32], sm[:, 32:48], sm[:, 48:64]
    STAB, BASE, LNB, ENEW = sm[:, 64:80], sm[:, 80:96], sm[:, 96:112], sm[:, 112:128]
    PM = sm[:, 128:144]
    LN = sm[0:NS, 144:160]
    PIO = sm[:, 160:161]
    MX = sm[0:16, 161:162]
    RDEN = sm[0:16, 162:163]
    DG = sm[0:16, 176:192]
    PIOi = k.IDX[:, 0:1]
    S.I("pool", "memset", ap=H[0:NS, 5120:5120 + 2048 + 256 + 1024], constant=1.0)
    S.I("pool", "affine_select", out=SELb, in_=SELb, pattern=[[1, NS], [0, P]], compare_op=ALU.is_equal, fill=0.0, base=0, channel_multiplier=-1)
    S.I("pool", "affine_select", out=ASEL, in_=ASEL, pattern=[[-1, NS], [1, NS]], compare_op=ALU.is_equal, fill=0.0, base=0, channel_multiplier=0)
    S.I("pool", "affine_select", out=DMb, in_=DMb, pattern=[[1, 16], [0, 64]], compare_op=ALU.is_equal, fill=0.0, base=0, channel_multiplier=-1)
    S.dma("sp", k.TOK[0:32, 0:128], I["t5_oh15"])
    S.dma("sp", k.TOK[0:32, 128:144], I["rel_bias"])
    p15 = k.ps()
    S.I("pe", "matmul", out=p15[:, 0:16], lhsT=k.TOK[0:32, 0:128], rhs=k.TOK[0:32, 128:144], start=True, stop=True)
    S.I("act", "activation", out=BS15, in_=p15[:, 0:16], func=AF.Copy)
    S.dma("sp", rb31b, I["rel_bias"][31:32, :].broadcast_to([P, 16]))
    S.dma("sp", rb0b, I["rel_bias"][0:1, :].broadcast_to([P, 16]))
    S.I("dve", "tensor_tensor", out=DB15, in0=BS15, in1=rb31b, op=ALU.subtract)
    S.I("pool", "iota", k.IDX[:, 0:1], pattern=[[0, 1]], base=0, channel_multiplier=1, writes=[k.IDX[:, 0:1]])
    S.I("dve", "tensor_copy", out=PIO, in_=k.IDX[:, 0:1])
    S.dma("sp", k.IDX[:], I["page_table"].broadcast_to([P, NS * 16]))
    S.I("dve", "tensor_scalar", out=k.IDX[:], in0=k.IDX[:], scalar1=128.0, scalar2=PIO, op0=ALU.mult, op1=ALU.add)
    S.I("dve", "tensor_copy", out=QSb, in_=QS)
    S.I("dve", "tensor_copy", out=VSb, in_=VS)
    S.I("dve", "tensor_tensor", out=k.TOK[0:NS, :], in0=QS, in1=KS, op=ALU.mult)
    S.I("dve", "tensor_reduce", out=LN, in_=k.TOK[0:NS, :].rearrange("p (h d) -> p h d", h=16), axis=AX.X, op=ALU.add)

    def gather(si, g):
        for jj in range(4):
            col = si * 16 + g * 4 + jj
            off = bass.IndirectOffsetOnAxis(ap=k.IDX[:, col:col + 1], axis=0)
            S.dma("pool", KSL[g % 2][:, jj, :], I["cache_k"], meth="indirect_dma_start", extra_reads=[k.IDX[:, col:col + 1]],
                  out_offset=None, in_offset=off)

    def vgather(si, j):
        col = si * 16 + j
        off = bass.IndirectOffsetOnAxis(ap=k.IDX[:, col:col + 1], axis=0)
        S.dma("pool", VPG[j % 4], I["cache_v"], meth="indirect_dma_start", extra_reads=[k.IDX[:, col:col + 1]],
              out_offset=None, in_offset=off)

    gather(0, 0)
    for si in range(NS):
        for hh in range(2):
            pq = k.ps()
            S.I("pe", "matmul", out=pq[:], lhsT=SELb[:, si, :], rhs=QSb[:, hh * 512:(hh + 1) * 512], start=True, stop=True)
            S.I("act", "activation", out=QB[:, hh * 512:(hh + 1) * 512], in_=pq[:], func=AF.Copy)
        LNm = sm[0:NS, 192:208]
        S.I("dve", "tensor_scalar", out=LNm, in0=LN, scalar1=k.ident_f[0:NS, si:si + 1], scalar2=None, op0=ALU.mult)
        pln = k.ps()
        S.I("pe", "matmul", out=pln[:, 0:16], lhsT=k.ones_f[0:NS, :], rhs=LNm, start=True, stop=True)
        S.I("act", "activation", out=LNB, in_=pln[:, 0:16], func=AF.Copy)
        for g in range(4):
            if g < 3:
                gather(si, g + 1)
            elif si + 1 < NS:
                gather(si + 1, 0)
            for jj in range(4):
                j = g * 4 + jj
                S.I("dve", "tensor_tensor", out=k.TOK[:], in0=KSL[g % 2][:, jj, :], in1=QB, op=ALU.mult)
                S.I("dve", "tensor_reduce", out=L[:, j, :], in_=k.TOK[:].rearrange("p (h d) -> p h d", h=16), axis=AX.X, op=ALU.add)
        for j in range(4):
            vgather(si, j)
        pc = k.ps()
        S.I("pe", "matmul", out=pc[:, 0:256], lhsT=k.ones_f[:], rhs=k.TOK2[:, 0:256], start=True, stop=True)
        S.I("act", "activation", out=COL, in_=pc[:, 0:256], func=AF.Copy)
        colv = COL.rearrange("p (n two h) -> p n two h", n=8, two=2)
        S.I("dve", "tensor_tensor", out=GS.rearrange("p h n -> p n h"), in0=colv[:, :, 0, :], in1=colv[:, :, 1, :], op=ALU.add)
        for h in range(16):
            S.I("dve", "max", out=k.stat[:, 24:32], in_=GS[:, h, :])
            S.I("dve", "tensor_scalar", out=MS[:, h, :], in0=GS[:, h, :], scalar1=k.stat[:, 26:27], scalar2=NEG, op0=ALU.is_lt, op1=ALU.mult)
        S.I("dve", "tensor_reduce", out=PM, in_=L.rearrange("p j h -> p h j"), axis=AX.X, op=ALU.max)
        ptm = k.ps()
        S.I("pe", "transpose", out=ptm[0:16, 0:P], in_=PM, identity=k.ident_f[:])
        S.I("dve", "tensor_reduce", out=MX, in_=ptm[0:16, 0:P], axis=AX.X, op=ALU.max)
        S.I("dve", "tensor_scalar", out=DG, in0=k.ident_f[0:16, 0:16], scalar1=MX, scalar2=0.125, op0=ALU.mult, op1=ALU.mult)
        pst = k.ps()
        S.I("pe", "matmul", out=pst[:, 0:16], lhsT=k.ones_f[0:16, :], rhs=DG, start=True, stop=True)
        S.I("act", "activation", out=STAB, in_=pst[:, 0:16], func=AF.Copy)
        S.I("dve", "tensor_tensor", out=BASE, in0=rb31b, in1=STAB, op=ALU.subtract)
        S.I("dve", "tensor_tensor", out=CADD.rearrange("p (n two) h -> p n two h", two=2),
            in0=BASE.unsqueeze(1).unsqueeze(1).broadcast_to([P, 8, 2, 16]),
            in1=MS.rearrange("p h n -> p n h").unsqueeze(2).broadcast_to([P, 8, 2, 16]), op=ALU.add)
        S.I("dve", "tensor_tensor", out=CADD[:, 15, :], in0=CADD[:, 15, :], in1=DB15, op=ALU.add)
        S.I("dve", "scalar_tensor_tensor", out=k.TOK2[:, 256:512], in0=k.TOK2[:, 0:256], scalar=0.125, in1=k.TOK2[:, 256:512], op0=ALU.mult, op1=ALU.add)
        S.I("act", "activation", out=H[:, 1024:1280], in_=k.TOK2[:, 256:512], func=AF.Exp)
        S.I("dve", "scalar_tensor_tensor", out=ENEW, in0=LNB, scalar=0.125, in1=rb0b, op0=ALU.mult, op1=ALU.add)
        S.I("dve", "tensor_tensor", out=ENEW, in0=ENEW, in1=STAB, op=ALU.subtract)
        S.I("act", "activation", out=ENEW, in_=ENEW, func=AF.Exp)
        S.I("dve", "tensor_scalar", out=ES, in0=ENEW[0:NS, :], scalar1=k.ident_f[0:NS, si:si + 1], scalar2=None, op0=ALU.mult)
        pacc = [k.ps(), k.ps()]
        pden = k.ps()
        for j in range(16):
            vb = VPG[j % 4]
            for hh in range(2):
                S.I("pe", "matmul", out=pacc[hh][0:16, :], lhsT=EB[:, j, :], rhs=vb[:, hh * 512:(hh + 1) * 512], start=(j == 0), stop=False)
            S.I("pe", "matmul", out=pden[0:16, 0:1], lhsT=EB[:, j, :], rhs=k.ones_bf[:, 0:1], start=(j == 0), stop=False)
            if j + 4 < 16:
                vgather(si, j + 4)
        for hh in range(2):
            S.I("pe", "matmul", out=pacc[hh][0:16, :], lhsT=ES, rhs=VSb[:, hh * 512:(hh + 1) * 512], start=False, stop=True)
        S.I("pe", "matmul", out=pden[0:16, 0:1], lhsT=ES, rhs=k.ones_bf[0:NS, 0:1], start=False, stop=True)
        S.I("act", "activation", out=sm[0:16, 163:164], in_=pden[0:16, 0:1], func=AF.Copy)
        S.I("dve", "reciprocal", out=RDEN, in_=sm[0:16, 163:164])
        for hh in range(2):
            S.I("dve", "scalar_tensor_tensor", out=MK[:, hh * 512:(hh + 1) * 512], in0=pacc[hh][0:16, :], scalar=RDEN,
                in1=H[0:16, 7424 + hh * 512:7424 + (hh + 1) * 512], op0=ALU.mult, op1=ALU.mult)
        po = k.ps()
        for c in range(DC):
            S.I("pe", "matmul", out=po[:, c * NS:(c + 1) * NS], lhsT=MK[:, c * P:(c + 1) * P], rhs=ASEL[:, si, :], start=True, stop=True)
        S.I("dve", "tensor_copy", out=k.OSMP[:, :, si], in_=po[:, 0:DC * NS].rearrange("p (c m) -> p c m", c=DC)[:, :, si])


def moba_out_proj(k, OT):
    S, nc, I = k.S, k.nc, k.I
    Wo = k.WA[:, 0:8192].rearrange("p (c n) -> p c n", c=DC)
    S.dma("pool", Wo, I["moba_w_o"].rearrange("(c p) n -> p c n", p=P))
    for (n0, n) in TT:
        for m in range(DC):
            po = k.ps()
            for c in range(DC):
                S.I("pe", "matmul", out=po[:, 0:n], lhsT=Wo[:, c, m * P:(m + 1) * P], rhs=OT[:, c, n0:n0 + n],
                    start=(c == 0), stop=(c == DC - 1))
            S.I("dve", "tensor_tensor", out=k.XT[:, m, n0:n0 + n], in0=po[:, 0:n], in1=k.XT[:, m, n0:n0 + n], op=ALU.add)


def rms_rstd_all(k, RS):
    S = k.S
    for (n0, n) in TT:
        S.I("act", "activation", out=k.SQ[:, :, 0:n], in_=k.XT[:, :, n0:n0 + n], func=AF.Square)
        pt = k.ps()
        for c in range(DC):
            S.I("pe", "matmul", out=pt[:, 0:n], lhsT=k.ones_bf[:], rhs=k.SQ[:, c, 0:n], start=(c == 0), stop=(c == DC - 1))
        S.I("act", "activation", out=RS[:, n0:n0 + n], in_=pt[:, 0:n], func=AF.Sqrt, scale=1.0 / D, bias=k.small[:, 0:1])
        S.I("dve", "reciprocal", out=RS[:, n0:n0 + n], in_=RS[:, n0:n0 + n])


def layer2_pool(k):
    S, nc = k.S, k.nc
    I, O = k.I, k.O
    RS = k.SCRF[:, 0:NT]
    rms_rstd_all(k, RS)
    g2 = k.gmix[:, 2, :]
    HIDF = k.HID[:].bitcast(F32)
    WAF = k.WA[:].bitcast(F32)
    PV = WAF[:, 0:1920].rearrange("p (c s r) -> p c s r", c=DC, s=NS)
    HL = k.SCRF[:, 2064:2064 + 256].rearrange("p (c t) -> p c t", c=DC)
    RCi = k.IDX[:, 0:16]
    RC = k.SM2[:, 0:64].rearrange("p (g t) -> p g t", g=4)
    PSW = k.SM2[:, 64:80]
    T15 = k.SM2[:, 80:96]
    psc = k.SM2[:, 96:104]
    SL = dict(allow_slow_non_contiguous=True)
    S.dma("sp", k.TOK2[0:8, 0:P], I["pool_scale"].rearrange("o (c p) -> (o c) p", p=P))
    ptp = k.ps()
    S.I("pe", "transpose", out=ptp[:, 0:8], in_=k.TOK2[0:8, 0:P], identity=k.ident_f[0:8, 0:8])
    S.I("act", "activation", out=psc, in_=ptp[:, 0:8], func=AF.Copy)
    S.I("pool", "iota", RCi, pattern=[[1, 16]], base=1, channel_multiplier=0, writes=[RCi])
    for g in range(4):
        S.I("dve", "tensor_scalar", out=RC[:, g, :], in0=RCi, scalar1=float(2 ** (g + 1)), scalar2=None, op0=ALU.min)
    S.I("dve", "reciprocal", out=k.SM2[:, 0:64], in_=k.SM2[:, 0:64])
    stp = I["st_pool"].rearrange("s r d -> (s r) d")
    for half in range(2):
        S.dma("sp", k.TOK[0:120, :], stp[half * 120:(half + 1) * 120, :])
        for cg in range(2):
            pt = k.ps()
            for j in range(4):
                c = cg * 4 + j
                S.I("pe", "transpose", out=pt[:, j * 120:(j + 1) * 120], in_=k.TOK[0:120, c * P:(c + 1) * P], identity=k.ident_f[0:120, 0:120])
            S.I("act", "activation", out=PV[:, cg * 4:(cg + 1) * 4, half * 8:(half + 1) * 8, :],
                in_=pt[:, 0:480].rearrange("p (c s r) -> p c s r", c=4, s=8), func=AF.Copy)
    S.dma("sp", O["pool_s"][:, 0:14, :], I["st_pool"][:, 1:15, :])
    for c in range(DC):
        S.I("dve", "scalar_tensor_tensor", out=HL[:, c, :], in0=k.XT[:, c, NT - 32:NT], scalar=g2[:, c:c + 1], in1=RS[:, NT - 32:NT],
            op0=ALU.mult, op1=ALU.mult)
    for half in range(2):
        pt = k.ps()
        for j in range(4):
            c = half * 4 + j
            S.I("pe", "transpose", out=pt[0:32, j * P:(j + 1) * P], in_=HL[:, c, :], identity=k.ident_f[:])
        S.I("act", "activation", out=k.TOK2[0:32, half * 512:(half + 1) * 512], in_=pt[0:32, :], func=AF.Copy)
    S.dma("sp", O["pool_p"], k.TOK2[1:16, :])
    S.dma("sp", O["pool_s"][:, 14, :], k.TOK2[16:32, :])
    A = HIDF[:, 0:4160].rearrange("p (c t) -> p c t", c=2)
    B = HIDF[:, 4160:8320].rearrange("p (c t) -> p c t", c=2)
    DB = k.HID[:, 16640:16640 + 4128].rearrange("p (c t) -> p c t", c=2)
    S.I("pool", "memset", ap=A[:, :, 0:16], constant=0.0)
    S.I("pool", "memset", ap=B[:, :, 0:16], constant=0.0)
    for g in range(4):
        win = 2 ** (g + 1)
        Wg = k.WA[:, 4096 + (g % 2) * 512:4096 + (g % 2) * 512 + 512].rearrange("p (kc n) -> p kc n", kc=2)
        S.dma("pool", Wg, I["pool_w"][g].rearrange("(kc p) n -> p kc n", p=P))

        def fill_h(buf):
            for cc in range(2):
                c = 2 * g + cc
                for (n0, n) in TT:
                    S.I("dve", "scalar_tensor_tensor", out=buf[:, cc, 16 + n0:16 + n0 + n], in0=k.XT[:, c, n0:n0 + n], scalar=g2[:, c:c + 1],
                        in1=RS[:, n0:n0 + n], op0=ALU.mult, op1=ALU.mult)

        fill_h(A)
        src, dst = A, B
        for i in range(g + 1):
            sh = 2 ** i
            for (n0, n) in TT[0:4]:
                S.I("dve", "tensor_tensor", out=dst[:, :, 16 + n0:16 + n0 + n], in0=src[:, :, 16 + n0:16 + n0 + n],
                    in1=src[:, :, 16 + n0 - sh:16 + n0 - sh + n], op=ALU.add)
            src, dst = dst, src
        R, T1 = src, dst
        fill_h(T1)
        for cc in range(2):
            c = 2 * g + cc
            for (n0, n) in TT[0:4]:
                S.I("dve", "scalar_tensor_tensor", out=DB[:, cc, n0:n0 + n], in0=R[:, cc, 16 + n0:16 + n0 + n], scalar=1.0 / win,
                    in1=T1[:, cc, 16 + n0:16 + n0 + n], op0=ALU.mult, op1=ALU.subtract)
            S.I("dve", "tensor_tensor", out=T15[:, 0:15], in0=R[:, cc, 16:31], in1=RC[:, g, 0:15], op=ALU.mult)
            S.I("dve", "tensor_tensor", out=DB[:, cc, 0:15], in0=T15[:, 0:15], in1=T1[:, cc, 16:31], op=ALU.subtract)
            S.I("dve", "tensor_reduce", out=PSW, in_=PV[:, c, :, 16 - win:15], axis=AX.X, op=ALU.add)
            S.I("dve", "tensor_tensor", out=PSW, in0=PSW, in1=T1[:, cc, 16 + TP:16 + NT], op=ALU.add)
            S.I("dve", "scalar_tensor_tensor", out=DB[:, cc, TP:NT], in0=PSW, scalar=1.0 / win, in1=T1[:, cc, 16 + TP:16 + NT],
                op0=ALU.mult, op1=ALU.subtract)
        for (n0, n) in TT:
            for m in range(2):
                po = k.ps()
                for kc in range(2):
                    S.I("pe", "matmul", out=po[:, 0:n], lhsT=Wg[:, kc, m * P:(m + 1) * P], rhs=DB[:, kc, n0:n0 + n], start=(kc == 0), stop=(kc == 1))
                c = 2 * g + m
                S.I("dve", "scalar_tensor_tensor", out=k.XT[:, c, n0:n0 + n], in0=po[:, 0:n], scalar=psc[:, c:c + 1], in1=k.XT[:, c, n0:n0 + n],
                    op0=ALU.mult, op1=ALU.add)
    ffn(k, 2)


RW_VEC = ["mu0", "mu1", "mu2", "mu3", "mu4", "mu5", "w0", "a0", "k_k", "k_a", "r_k", "lnx_g", "lnx_b"]


def layer3_rwkv(k):
    S, nc = k.S, k.nc
    I, O = k.I, k.O
    RS = k.SCRF[:, 0:NT]
    rms_rstd_all(k, RS)
    g3 = k.gmix[:, 3, :]
    k.RV = k.sbt("RV", [P, 13, DC], F32)
    RV = k.RV
    vec = lambda name: RV[:, RW_VEC.index(name), :]
    S.dma("sp", k.TOK2[0:104, 0:P], I["rw_vecs"])
    ptp = k.ps()
    S.I("pe", "transpose", out=ptp[:, 0:104], in_=k.TOK2[0:104, 0:P], identity=k.ident_f[0:104, 0:104])
    S.I("act", "activation", out=RV[:].rearrange("p v c -> p (v c)"), in_=ptp[:, 0:104], func=AF.Copy)
    SH = k.SM2[:, 0:128].rearrange("p (c s) -> p c s", c=DC)
    S.dma("sp", k.TOK[0:NS, :], I["st_shift"])
    pts = k.ps()
    for c in range(DC):
        S.I("pe", "transpose", out=pts[:, c * NS:(c + 1) * NS], in_=k.TOK[0:NS, c * P:(c + 1) * P], identity=k.ident_f[0:NS, 0:NS])
    S.I("act", "activation", out=k.SM2[:, 0:128], in_=pts[:, 0:128], func=AF.Copy)
    HL = k.SCRF[:, 2064:2064 + 256].rearrange("p (c t) -> p c t", c=DC)
    for c in range(DC):
        S.I("dve", "scalar_tensor_tensor", out=HL[:, c, :], in0=k.XT[:, c, NT - 32:NT], scalar=g3[:, c:c + 1], in1=RS[:, NT - 32:NT],
            op0=ALU.mult, op1=ALU.mult)
    for half in range(2):
        pt = k.ps()
        for j in range(4):
            c = half * 4 + j
            S.I("pe", "transpose", out=pt[0:32, j * P:(j + 1) * P], in_=HL[:, c, :], identity=k.ident_f[:])
        S.I("act", "activation", out=k.TOK2[0:32, half * 512:(half + 1) * 512], in_=pt[0:32, :], func=AF.Copy)
    S.dma("sp", O["shift_p"], k.TOK2[15:16, :])
    S.dma("sp", O["shift_s"], k.TOK2[16:32, :])

    DS = {nm: nc.dram_tensor("rw_" + nm, [D, NT], F32, kind="Internal").ap() for nm in ("r", "k", "v", "sg", "a", "g")}
    k.DS = DS
    HC = k.SCRF[:, 2320:2320 + 1040]
    MIXA = k.HT
    MIXB = k.HID[:, 0:DC * NT].rearrange("p (c n) -> p c n", c=DC)
    STG = [k.TOK[:, 0:512], k.TOK[:, 512:1024], k.TOK2[:, 0:512], k.TOK2[:, 512:1024]]
    stg_i = [0]

    def mixes(mlist):
        HCb = k.SCRF[:, 2320:2320 + 1033]
        XXb = k.SCRF[:, 3353:3353 + 528]
        for c in range(DC):
            for hb in range(2):
                t0 = hb * 1024
                if hb == 0:
                    S.I("pool", "memset", ap=HCb[:, 0:1], constant=0.0)
                else:
                    S.I("pool", "tensor_copy", out=HCb[:, 0:1], in_=HCb[:, 1024:1025])
                for q in range(2):
                    n0 = t0 + q * 512
                    S.I("dve", "scalar_tensor_tensor", out=HCb[:, 1 + q * 512:1 + (q + 1) * 512], in0=k.XT[:, c, n0:n0 + 512],
                        scalar=g3[:, c:c + 1], in1=RS[:, n0:n0 + 512], op0=ALU.mult, op1=ALU.mult)
                for q in range(2):
                    n0 = t0 + q * 512
                    S.I("dve", "tensor_tensor", out=XXb[:, 0:512], in0=HCb[:, q * 512:q * 512 + 512], in1=HCb[:, 1 + q * 512:1 + q * 512 + 512], op=ALU.subtract)
                    for (mi, dst) in mlist:
                        S.I("dve", "scalar_tensor_tensor", out=dst[:, c, n0:n0 + 512], in0=XXb[:, 0:512], scalar=RV[:, mi, c:c + 1],
                            in1=HCb[:, 1 + q * 512:1 + q * 512 + 512], op0=ALU.mult, op1=ALU.add)
            S.I("dve", "scalar_tensor_tensor", out=HCb[:, 0:NS], in0=k.XT[:, c, TP:NT], scalar=g3[:, c:c + 1], in1=RS[:, TP:NT], op0=ALU.mult, op1=ALU.mult)
            S.I("dve", "tensor_tensor", out=XXb[:, 0:NS], in0=SH[:, c, :], in1=HCb[:, 0:NS], op=ALU.subtract)
            for (mi, dst) in mlist:
                S.I("dve", "scalar_tensor_tensor", out=dst[:, c, TP:NT], in0=XXb[:, 0:NS], scalar=RV[:, mi, c:c + 1], in1=HCb[:, 0:NS],
                    op0=ALU.mult, op1=ALU.add)

    def store(ps_ap, n, dst_rows, n0, func=AF.Copy, bias=None):
        st_ = STG[stg_i[0] % 4]
        stg_i[0] += 1
        if bias is None:
            S.I("act", "activation", out=st_[:, 0:n], in_=ps_ap, func=func)
        else:
            S.I("dve", "tensor_scalar", out=st_[:, 0:n], in0=ps_ap, scalar1=bias, scalar2=None, op0=ALU.add)
            S.I("act", "activation", out=st_[:, 0:n], in_=st_[:, 0:n], func=func)
        S.dma("sp", dst_rows[:, n0:n0 + n], st_[:, 0:n])

    def full_proj(wname, mix, dst):
        W = k.WA[:, 0:8192].rearrange("p (c n) -> p c n", c=DC) if wname in ("rwkv_w_r", "rwkv_w_v") else k.WA[:, 8192:16384].rearrange("p (c n) -> p c n", c=DC)
        S.dma("pool", W, I[wname].rearrange("(c p) n -> p c n", p=P))
        for (n0, n) in TT:
            for m in range(DC):
                po = k.ps()
                for c in range(DC):
                    S.I("pe", "matmul", out=po[:, 0:n], lhsT=W[:, c, m * P:(m + 1) * P], rhs=mix[:, c, n0:n0 + n], start=(c == 0), stop=(c == DC - 1))
                store(po[:, 0:n], n, dst[m * P:(m + 1) * P, :], n0)

    def lora_proj(w1name, w2name, r, mix, dst, mid_func, out_func, out_bias):
        LW = k.WA[:, 0:8192] if w1name != "rwkv_a1" else k.WA[:, 8192:16384]
        W1 = LW[:, 0:DC * r].rearrange("p (c n) -> p c n", c=DC)
        W2 = LW[0:r, 1024:2048]
        MID = LW[0:r, 2048:2048 + 512]
        S.dma("pool", W1, I[w1name].rearrange("(c p) n -> p c n", p=P))
        S.dma("pool", W2, I[w2name])
        for (n0, n) in TT:
            pm = k.ps()
            for c in range(DC):
                S.I("pe", "matmul", out=pm[0:r, 0:n], lhsT=W1[:, c, :], rhs=mix[:, c, n0:n0 + n], start=(c == 0), stop=(c == DC - 1))
            S.I("act", "activation", out=MID[:, 0:n], in_=pm[0:r, 0:n], func=mid_func)
            for m in range(DC):
                po = k.ps()
                S.I("pe", "matmul", out=po[:, 0:n], lhsT=W2[:, m * P:(m + 1) * P], rhs=MID[:, 0:n], start=True, stop=True)
                store(po[:, 0:n], n, dst[m * P:(m + 1) * P, :], n0, func=out_func, bias=(None if out_bias is None else out_bias[:, m:m + 1]))

    mixes([(0, MIXA), (2, MIXB)])
    full_proj("rwkv_w_r", MIXA, DS["r"])
    full_proj("rwkv_w_k", MIXB, DS["k"])
    mixes([(3, MIXA), (1, MIXB)])
    full_proj("rwkv_w_v", MIXA, DS["v"])
    lora_proj("rwkv_w1", "rwkv_w2", 64, MIXB, DS["sg"], AF.Tanh, AF.Sigmoid, vec("w0"))
    mixes([(4, MIXA), (5, MIXB)])
    lora_proj("rwkv_a1", "rwkv_a2", 64, MIXA, DS["a"], AF.Copy, AF.Sigmoid, vec("a0"))
    lora_proj("rwkv_g1", "rwkv_g2", 128, MIXB, DS["g"], AF.Sigmoid, AF.Copy, None)
    rwkv_stage2(k)


RW_EPS = 64e-5
EXPM05 = 0.6065306597126334


def rwkv_stage2(k):
    S, nc = k.S, k.nc
    I, O = k.I, k.O
    DS = k.DS
    RV = k.RV
    col = lambda name, c: RV[:, RW_VEC.index(name), c:c + 1]
    HTF = k.HT[:].rearrange("p c n -> p (c n)").bitcast(F32)
    WAF = k.WA[:].bitcast(F32)
    HIDF = k.HID[:].bitcast(F32)
    OT = k.HID[:, 0:DC * NT].rearrange("p (c n) -> p c n", c=DC)
    Bf = [HTF[:, i * 2048:(i + 1) * 2048] for i in range(4)] + [WAF[:, i * 2048:(i + 1) * 2048] for i in range(4)] + \
         [k.SCRF[:, 0:2048], k.SCRF[:, 2048:4096]]
    TL = HIDF[:, 8256:11440]
    sm = k.SM2
    BLK = TL[:, 2560:2688]
    MSK = TL[0:64, 2688:2880].rearrange("p (m t) -> p m t", m=3)
    S.I("pool", "memset", ap=BLK, constant=0.0)
    S.I("pool", "memset", ap=BLK[0:64, 0:64], constant=1.0)
    S.I("pool", "memset", ap=BLK[64:128, 64:128], constant=1.0)
    S.I("pool", "memset", ap=TL[0:64, 2688:2880], constant=1.0)
    S.I("pool", "affine_select", out=MSK[:, 0, :], in_=MSK[:, 0, :], pattern=[[1, 64]], compare_op=ALU.is_ge, fill=0.0, base=-1, channel_multiplier=-1)
    S.I("pool", "affine_select", out=MSK[:, 1, :], in_=MSK[:, 1, :], pattern=[[-1, 64]], compare_op=ALU.is_ge, fill=0.0, base=-1, channel_multiplier=1)
    S.I("pool", "affine_select", out=MSK[:, 2, :], in_=MSK[:, 2, :], pattern=[[1, 64]], compare_op=ALU.is_ge, fill=0.0, base=0, channel_multiplier=-1)
    I64 = k.ident_f[0:64, 0:64]
    SMP = TL[:, 2880:3040].rearrange("p (a s) -> p a s", a=10)
    OMK = sm[:, 200:201]
    wkv_p_v = O["wkv_p"].rearrange("h i j -> (h i) j")
    rwsm = nc.dram_tensor("rw_smp", [5, NS, D], F32, kind="Internal").ap()

    def q8(ap512):
        return ap512.rearrange("p (q t) -> p q t", q=8)

    for c in range(DC):
        rows = slice(c * P, (c + 1) * P)
        R_, K_, V_, W_, A_ = Bf[0], Bf[1], Bf[2], Bf[3], Bf[4]
        for i, nm in enumerate(("r", "k", "v", "sg", "a")):
            S.dma("sp", Bf[i], DS[nm][rows, 0:TP])
            S.dma("sp", SMP[:, i, :], DS[nm][rows, TP:NT])
        S.I("dve", "tensor_scalar", out=OMK, in0=col("k_a", c), scalar1=-1.0, scalar2=1.0, op0=ALU.mult, op1=ALU.add)
        sR, sK, sV, sW, sA, sKK, sT, sB = (SMP[:, i, :] for i in range(8))
        S.I("dve", "tensor_scalar", out=W_, in0=W_, scalar1=EXPM05, scalar2=None, op0=ALU.mult)
        S.I("dve", "tensor_scalar", out=sW, in0=sW, scalar1=EXPM05, scalar2=None, op0=ALU.mult)
        NEW, LG, KK, TMP = Bf[5], Bf[7], Bf[8], Bf[9]
        S.I("dve", "tensor_scalar", out=NEW, in0=W_, scalar1=-1.0, scalar2=None, op0=ALU.mult)
        for ch in range(32):
            sl = slice(ch * 64, (ch + 1) * 64)
            S.I("dve", "tensor_tensor_scan", out=LG[:, sl], data0=k.ones_f[:, 0:64], data1=NEW[:, sl], initial=0.0, op0=ALU.mult, op1=ALU.add)
        for (n0, n, kk_ap, k_ap, tmp_ap) in [(q * 512, 512, KK[:, q * 512:(q + 1) * 512], K_[:, q * 512:(q + 1) * 512], TMP[:, q * 512:(q + 1) * 512]) for q in range(4)] + \
                [(0, NS, sKK, sK, sT)]:
            S.I("dve", "tensor_scalar", out=kk_ap, in0=k_ap, scalar1=col("k_k", c), scalar2=None, op0=ALU.mult)
            S.I("act", "activation", out=tmp_ap, in_=kk_ap, func=AF.Square)
            pn = k.ps()
            S.I("pe", "matmul", out=pn[:, 0:n], lhsT=BLK, rhs=tmp_ap, start=True, stop=True)
            S.I("act", "activation", out=tmp_ap, in_=pn[:, 0:n], func=AF.Sqrt)
            S.I("dve", "tensor_scalar", out=tmp_ap, in0=tmp_ap, scalar1=1e-12, scalar2=None, op0=ALU.max)
            S.I("dve", "reciprocal", out=tmp_ap, in_=tmp_ap)
            S.I("dve", "tensor_tensor", out=kk_ap, in0=kk_ap, in1=tmp_ap, op=ALU.mult)
        for (k_ap, a_ap, kk_ap, tmp_ap) in ((K_, A_, KK, TMP), (sK, sA, sKK, sT)):
            S.I("dve", "tensor_scalar", out=tmp_ap, in0=a_ap, scalar1=col("k_a", c), scalar2=OMK, op0=ALU.mult, op1=ALU.add)
            S.I("dve", "tensor_tensor", out=k_ap, in0=k_ap, in1=tmp_ap, op=ALU.mult)
            S.I("dve", "tensor_tensor", out=a_ap, in0=kk_ap, in1=a_ap, op=ALU.mult)
        B_ = A_
        S.I("dve", "tensor_tensor", out=TMP, in0=LG, in1=W_, op=ALU.add)
        S.I("act", "activation", out=TMP, in_=TMP, func=AF.Exp)
        S.I("dve", "scalar_tensor_tensor", out=KK, in0=KK, scalar=-1.0, in1=TMP, op0=ALU.mult, op1=ALU.mult)
        EP = NEW
        S.I("act", "activation", out=EP, in_=LG, func=AF.Exp)
        S.I("act", "activation", out=LG, in_=LG, func=AF.Exp, scale=-1.0)
        S.I("dve", "tensor_tensor", out=R_, in0=R_, in1=EP, op=ALU.mult)
        S.I("dve", "tensor_tensor", out=K_, in0=K_, in1=LG, op=ALU.mult)
        S.I("dve", "tensor_tensor", out=B_, in0=B_, in1=LG, op=ALU.mult)
        At_, Kt_, Bt_, Rt_ = KK, K_, B_, R_
        GCs = sm[:, 208:240]
        S.I("dve", "tensor_copy", out=GCs, in_=EP.rearrange("p (c t) -> p c t", c=32)[:, :, 63])
        pg1 = k.ps()
        S.I("pe", "transpose", out=pg1[0:32, 0:P], in_=GCs, identity=k.ident_f[:])
        GCT = sm[0:32, 0:128]
        S.I("act", "activation", out=GCT, in_=pg1[0:32, 0:P], func=AF.Copy)
        pg2 = k.ps()
        for hp in range(2):
            S.I("pe", "transpose", out=pg2[0:64, hp * 32:(hp + 1) * 32], in_=GCT[:, hp * 64:(hp + 1) * 64], identity=k.ident_f[0:32, 0:32])
        GC0 = sm[0:64, 128:192].rearrange("p (h c) -> p h c", h=2)
        S.I("act", "activation", out=sm[0:64, 128:192], in_=pg2[0:64, 0:64], func=AF.Copy)
        rwkv_sample_step(k, c, SMP, BLK, rwsm, Bf, OT)
        S.dma("sp", k.TOK[0:64, 896:1024], I["rwkv_lnx"][0:1, rows].broadcast_to([64, P]))
        S.dma("sp", k.TOK2[0:64, 896:1024], I["rwkv_lnx"][1:2, rows].broadcast_to([64, P]))
        LNXG, LNXB = k.TOK[0:64, 896:1024], k.TOK2[0:64, 896:1024]
        f3, f5, f7, f9 = Bf[3], Bf[5], Bf[7], Bf[9]
        t64 = lambda buf, i: buf[0:64, i * 512:(i + 1) * 512]
        Aba, AbaT, Aka, Akr = (t64(f3, i) for i in range(4))
        Abr, T1, P2, P2T = (t64(f5, i) for i in range(4))
        T2, ATt, BTt, KTt = (t64(f7, i) for i in range(4))
        VTt, RTt, W0T, U0T = (t64(f9, i) for i in range(4))
        X6 = Bf[6]
        AHT, AH, RH, GTt = (t64(X6, i) for i in range(4))
        NPr, O0T, OTK, SQ2 = (TL[0:64, i * 512:(i + 1) * 512] for i in range(4))
        GST = TL[:, 2048:2048 + 256]
        BON = TL[:, 2304:2304 + 256]
        S0T = TL[0:64, 3040:3168].rearrange("p (h i) -> p h i", h=2)
        S.I("pool", "memset", ap=TL[0:64, 3040:3168], constant=0.0)
        for bt in range(8):
            t0 = bt * 256
            for (src, dst) in ((At_, ATt), (Bt_, BTt), (Kt_, KTt), (V_, VTt), (Rt_, RTt)):
                pt = k.ps()
                for cl in range(4):
                    S.I("pe", "transpose", out=pt[0:64, cl * P:(cl + 1) * P], in_=src[:, t0 + cl * 64:t0 + (cl + 1) * 64], identity=k.ident_f[:])
                S.I("act", "activation", out=dst, in_=pt[0:64, :], func=AF.Copy)
            tm = lambda X, cl, hp: X.rearrange("p (c f) -> p c f", c=4)[:, cl, hp * 64:(hp + 1) * 64]
            fm = lambda X, cl, hp: X[hp * 64:(hp + 1) * 64, t0 + cl * 64:t0 + (cl + 1) * 64]
            tl = lambda X, cl, hp: X[:, (hp * 4 + cl) * 64:(hp * 4 + cl + 1) * 64]

            def stage(dst, fn, mask=None, add=None):
                banks = [k.ps(), k.ps()]
                for hp in range(2):
                    for cl in range(4):
                        mm = fn(cl, hp)
                        for i, (l, r) in enumerate(mm):
                            S.I("pe", "matmul", out=banks[hp][0:64, cl * 64:(cl + 1) * 64], lhsT=l, rhs=r, start=(i == 0), stop=(i == len(mm) - 1))
                for hp in range(2):
                    d = dst[:, hp * 256:(hp + 1) * 256]
                    if mask is not None:
                        S.I("dve", "tensor_tensor", out=d.rearrange("p (c t) -> p c t", c=4), in0=banks[hp][0:64, 0:256].rearrange("p (c t) -> p c t", c=4),
                            in1=mask.unsqueeze(1).broadcast_to([64, 4, 64]), op=ALU.mult)
                    elif add is not None:
                        S.I("dve", "tensor_tensor", out=d.rearrange("p (c t) -> p c t", c=4), in0=banks[hp][0:64, 0:256].rearrange("p (c t) -> p c t", c=4),
                            in1=add.unsqueeze(1).broadcast_to([64, 4, 64]), op=ALU.add)
                    elif hp == 0:
                        S.I("act", "activation", out=d, in_=banks[hp][0:64, 0:256], func=AF.Copy)
                    else:
                        S.I("dve", "tensor_copy", out=d, in_=banks[hp][0:64, 0:256])

            stage(Aba, lambda cl, hp: [(fm(Bt_, cl, hp), fm(At_, cl, hp))], mask=MSK[:, 0, :])
            stage(AbaT, lambda cl, hp: [(fm(At_, cl, hp), fm(Bt_, cl, hp))], mask=MSK[:, 1, :])
            stage(Aka, lambda cl, hp: [(fm(Kt_, cl, hp), fm(At_, cl, hp))], mask=MSK[:, 0, :])
            stage(Akr, lambda cl, hp: [(fm(Kt_, cl, hp), fm(Rt_, cl, hp))], mask=MSK[:, 2, :])
            stage(Abr, lambda cl, hp: [(fm(Bt_, cl, hp), fm(Rt_, cl, hp))], mask=MSK[:, 2, :])
            S.I("dve", "tensor_tensor", out=q8(T1), in0=q8(Aba), in1=I64.unsqueeze(1).broadcast_to([64, 8, 64]), op=ALU.add)
            Pc, PcT, Pn, PnT, Tc, Tn = Aba, AbaT, P2, P2T, T1, T2
            for it in range(5):
                stage(Pn, lambda cl, hp: [(tl(PcT, cl, hp), tl(Pc, cl, hp))])
                if it < 4:
                    stage(PnT, lambda cl, hp: [(tl(Pc, cl, hp), tl(PcT, cl, hp))])
                if it < 4:
                    stage(Tn, lambda cl, hp: [(tl(PnT, cl, hp), tl(Tc, cl, hp)), (I64, tl(Tc, cl, hp))])
                else:
                    stage(PnT, lambda cl, hp: [(tl(Pc, cl, hp), tl(PcT, cl, hp))])
                    stage(Tn, lambda cl, hp: [(tl(PnT, cl, hp), tl(Tc, cl, hp)), (I64, tl(Tc, cl, hp))])
                Pc, PcT, Pn, PnT = Pn, PnT, Pc, PcT
                Tc, Tn = Tn, Tc
            Tm = Tc
            stage(W0T, lambda cl, hp: [(tl(Aka, cl, hp), tm(VTt, cl, hp))])
            stage(U0T, lambda cl, hp: [(tl(Tm, cl, hp), tl(W0T, cl, hp))])
            stage(AHT, lambda cl, hp: [(tl(Tm, cl, hp), tm(ATt, cl, hp))])
            stage(AH, lambda cl, hp: [(tm(ATt, cl, hp), tl(Tm, cl, hp))])
            stage(RH, lambda cl, hp: [(tm(RTt, cl, hp), I64), (tl(AHT, cl, hp), tl(Abr, cl, hp))])
            stage(GTt, lambda cl, hp: [(tl(AHT, cl, hp), tm(BTt, cl, hp))], add=I64)
            stage(NPr, lambda cl, hp: [(tm(KTt, cl, hp), tm(VTt, cl, hp)), (tm(BTt, cl, hp), tl(U0T, cl, hp))])
            stage(O0T, lambda cl, hp: [(tl(Akr, cl, hp), tm(VTt, cl, hp)), (tl(Abr, cl, hp), tl(U0T, cl, hp))])
            for cl in range(4):
                ch = bt * 4 + cl
                po, pn_ = k.ps(), k.ps()
                for hp in range(2):
                    S.I("pe", "matmul", out=po[0:64, hp * 64:(hp + 1) * 64], lhsT=tl(RH, cl, hp), rhs=S0T[:, hp, :], start=True, stop=True)
                    S.I("pe", "matmul", out=pn_[0:64, hp * 64:(hp + 1) * 64], lhsT=tl(GTt, cl, hp), rhs=S0T[:, hp, :], start=True, stop=True)
                for hp in range(2):
                    S.I("dve", "tensor_tensor", out=OTK[:, (cl * 2 + hp) * 64:(cl * 2 + hp + 1) * 64], in0=po[0:64, hp * 64:(hp + 1) * 64], in1=tl(O0T, cl, hp), op=ALU.add)
                    S.I("dve", "tensor_tensor", out=S0T[:, hp, :], in0=pn_[0:64, hp * 64:(hp + 1) * 64], in1=tl(NPr, cl, hp), op=ALU.add)
                    S.I("dve", "tensor_scalar", out=S0T[:, hp, :], in0=S0T[:, hp, :], scalar1=GC0[:, hp, ch:ch + 1], scalar2=None, op0=ALU.mult)
            o8 = q8(OTK)
            st8 = sm[0:64, 240:248]
            st8b = sm[0:64, 248:256]
            S.I("dve", "tensor_reduce", out=st8, in_=o8, axis=AX.X, op=ALU.add)
            S.I("act", "activation", out=SQ2, in_=OTK, func=AF.Square)
            S.I("dve", "tensor_reduce", out=st8b, in_=q8(SQ2), axis=AX.X, op=ALU.add)
            S.I("dve", "tensor_scalar", out=st8, in0=st8, scalar1=1.0 / 64, scalar2=None, op0=ALU.mult)
            mm2 = sm[0:64, 192:200]
            S.I("dve", "tensor_tensor", out=mm2, in0=st8, in1=st8, op=ALU.mult)
            S.I("dve", "scalar_tensor_tensor", out=st8b, in0=st8b, scalar=1.0 / 64, in1=mm2, op0=ALU.mult, op1=ALU.subtract)
            S.I("act", "activation", out=st8b, in_=st8b, func=AF.Sqrt, bias=k.small[0:64, 2:3], scale=1.0)
            S.I("dve", "reciprocal", out=st8b, in_=st8b)
            S.I("dve", "tensor_tensor", out=o8, in0=o8, in1=st8.unsqueeze(2).broadcast_to([64, 8, 64]), op=ALU.subtract)
            S.I("dve", "tensor_tensor", out=o8, in0=o8, in1=st8b.unsqueeze(2).broadcast_to([64, 8, 64]), op=ALU.mult)
            o_c = OTK.rearrange("p (c f) -> p c f", c=4)
            S.I("dve", "tensor_tensor", out=o_c, in0=o_c, in1=LNXG.unsqueeze(1).broadcast_to([64, 4, P]), op=ALU.mult)
            S.I("dve", "tensor_tensor", out=o_c, in0=o_c, in1=LNXB.unsqueeze(1).broadcast_to([64, 4, P]), op=ALU.add)
            pf = k.ps()
            for cl in range(4):
                S.I("pe", "transpose", out=pf[:, cl * 64:(cl + 1) * 64], in_=OTK[:, cl * P:(cl + 1) * P], identity=I64)
            S.I("dve", "scalar_tensor_tensor", out=BON, in0=Rt_[:, t0:t0 + 256], scalar=col("r_k", c), in1=Kt_[:, t0:t0 + 256], op0=ALU.mult, op1=ALU.mult)
            pb = k.ps()
            S.I("pe", "matmul", out=pb[:, 0:256], lhsT=BLK, rhs=BON, start=True, stop=True)
            S.I("dve", "tensor_tensor", out=BON, in0=pb[:, 0:256], in1=V_[:, t0:t0 + 256], op=ALU.mult)
            S.I("dve", "tensor_tensor", out=BON, in0=pf[:, 0:256], in1=BON, op=ALU.add)
            S.dma("sp", GST, DS["g"][rows, t0:t0 + 256])
            S.I("dve", "tensor_tensor", out=OT[:, c, t0:t0 + 256], in0=BON, in1=GST, op=ALU.mult)
        pw = k.ps()
        for hp in range(2):
            S.I("pe", "transpose", out=pw[0:64, hp * 64:(hp + 1) * 64], in_=S0T[:, hp, :], identity=I64)
        S.I("act", "activation", out=SQ2[:, 0:128], in_=pw[0:64, 0:128], func=AF.Copy)
        for hp in range(2):
            S.dma("sp", O["wkv_p"][2 * c + hp], SQ2[:, hp * 64:(hp + 1) * 64])
    Wo = k.WA[:, 0:8192].rearrange("p (c n) -> p c n", c=DC)
    S.dma("pool", Wo, I["rwkv_w_o"].rearrange("(c p) n -> p c n", p=P))
    for (n0, n) in TT:
        for m in range(DC):
            po = k.ps()
            for c in range(DC):
                S.I("pe", "matmul", out=po[:, 0:n], lhsT=Wo[:, c, m * P:(m + 1) * P], rhs=OT[:, c, n0:n0 + n], start=(c == 0), stop=(c == DC - 1))
            S.I("dve", "tensor_tensor", out=k.XT[:, m, n0:n0 + n], in0=po[:, 0:n], in1=k.XT[:, m, n0:n0 + n], op=ALU.add)
    ffn(k, 3)


def rwkv_sample_step(k, c, SMP, BLK, rwsm, Bf, OT):
    S, nc = k.S, k.nc
    I, O = k.I, k.O
    RV = k.RV
    col = lambda name: RV[:, RW_VEC.index(name), c:c + 1]
    sR, sK, sV, sW, sA, sKK, sT, sB = (SMP[:, i, :] for i in range(8))
    sB = sA
    sDEC, sO = SMP[:, 8, :], SMP[:, 9, :]
    S.I("act", "activation", out=sDEC, in_=sW, func=AF.Exp, scale=-1.0)
    pt = k.ps()
    for i, src in enumerate((sKK, sDEC, sB, sK)):
        S.I("pe", "transpose", out=pt[0:NS, i * P:(i + 1) * P], in_=src, identity=k.ident_f[:])
    pt2 = k.ps()
    S.I("pe", "transpose", out=pt2[0:NS, 0:P], in_=sR, identity=k.ident_f[:])
    ROWS = k.TOK[0:NS, 0:640]
    S.I("act", "activation", out=k.TOK[0:NS, 0:512], in_=pt[0:NS, 0:512], func=AF.Copy)
    S.I("act", "activation", out=k.TOK[0:NS, 512:640], in_=pt2[0:NS, 0:P], func=AF.Copy)
    for i in range(5):
        S.dma("sp", rwsm[i][:, c * P:(c + 1) * P], k.TOK[0:NS, i * P:(i + 1) * P])
    SS = Bf[5][:, 0:1024].rearrange("p (s j) -> p s j", s=NS)
    XB = [Bf[7][:, 0:1024], Bf[7][:, 1024:2048], Bf[9][:, 0:1024], Bf[9][:, 1024:2048], Bf[5][:, 1024:2048]]
    XB = [x.rearrange("p (s j) -> p s j", s=NS) for x in XB]
    for i in range(5):
        for hp in range(2):
            src = rwsm[i][:, c * P + hp * 64:c * P + (hp + 1) * 64]
            S.dma("sp", XB[i][hp * 64:(hp + 1) * 64], src.unsqueeze(0).broadcast_to([64, NS, 64]))
    kkB, wB, bB, kB, rB = XB
    for hp in range(2):
        S.dma("sp", SS[hp * 64:(hp + 1) * 64], I["st_wkv"][:, 2 * c + hp].rearrange("s i j -> i s j"))
    TM = Bf[3][:, 0:1024].rearrange("p (s j) -> p s j", s=NS)
    sSA = SMP[:, 6, :]
    S.I("dve", "tensor_tensor", out=TM, in0=SS, in1=kkB, op=ALU.mult)
    S.I("dve", "tensor_reduce", out=sSA, in_=TM, axis=AX.X, op=ALU.add)
    S.I("dve", "tensor_tensor", out=SS, in0=SS, in1=wB, op=ALU.mult)
    S.I("dve", "tensor_tensor", out=TM, in0=bB, in1=sSA.unsqueeze(2).broadcast_to([P, NS, 64]), op=ALU.mult)
    S.I("dve", "tensor_tensor", out=SS, in0=SS, in1=TM, op=ALU.subtract)
    S.I("dve", "tensor_tensor", out=TM, in0=kB, in1=sV.unsqueeze(2).broadcast_to([P, NS, 64]), op=ALU.mult)
    S.I("dve", "tensor_tensor", out=SS, in0=SS, in1=TM, op=ALU.add)
    for hp in range(2):
        S.dma("sp", O["wkv_s"][:, 2 * c + hp].rearrange("s i j -> i s j"), SS[hp * 64:(hp + 1) * 64])
    S.I("dve", "tensor_tensor", out=TM, in0=SS, in1=rB, op=ALU.mult)
    S.I("dve", "tensor_reduce", out=sO, in_=TM, axis=AX.X, op=ALU.add)
    sm = k.SM2
    MEAN, VAR, SQ = sm[:, 0:16], sm[:, 16:32], sm[:, 32:48]
    p1 = k.ps()
    S.I("pe", "matmul", out=p1[:, 0:NS], lhsT=BLK, rhs=sO, start=True, stop=True)
    S.I("dve", "tensor_scalar", out=MEAN, in0=p1[:, 0:NS], scalar1=1.0 / 64, scalar2=None, op0=ALU.mult)
    S.I("dve", "tensor_tensor", out=sO, in0=sO, in1=MEAN, op=ALU.subtract)
    S.I("act", "activation", out=SQ, in_=sO, func=AF.Square)
    p2 = k.ps()
    S.I("pe", "matmul", out=p2[:, 0:NS], lhsT=BLK, rhs=SQ, start=True, stop=True)
    S.I("act", "activation", out=VAR, in_=p2[:, 0:NS], func=AF.Sqrt, scale=1.0 / 64, bias=k.small[:, 2:3])
    S.I("dve", "reciprocal", out=VAR, in_=VAR)
    S.I("dve", "tensor_tensor", out=sO, in0=sO, in1=VAR, op=ALU.mult)
    S.I("dve", "tensor_scalar", out=sO, in0=sO, scalar1=col("lnx_g"), scalar2=col("lnx_b"), op0=ALU.mult, op1=ALU.add)
    S.I("dve", "scalar_tensor_tensor", out=SQ, in0=sR, scalar=col("r_k"), in1=sK, op0=ALU.mult, op1=ALU.mult)
    p3 = k.ps()
    S.I("pe", "matmul", out=p3[:, 0:NS], lhsT=BLK, rhs=SQ, start=True, stop=True)
    S.I("dve", "tensor_tensor", out=SQ, in0=p3[:, 0:NS], in1=sV, op=ALU.mult)
    S.I("dve", "tensor_tensor", out=sO, in0=sO, in1=SQ, op=ALU.add)
    GS_ = sm[:, 48:64]
    S.dma("sp", GS_, k.DS["g"][c * P:(c + 1) * P, TP:NT])
    S.I("dve", "tensor_tensor", out=OT[:, c, TP:NT], in0=sO, in1=GS_, op=ALU.mult)


def ffn(k, li):
    S, nc = k.S, k.nc
    I, O = k.I, k.O
    rmsnorm(k, k.gffn[:, li, :], out_bf=k.HT)
    w_in = I["ffn_w_in"][li].rearrange("(c p) n -> p c n", p=P)
    w_out = I["ffn_w_out"][li].rearrange("(c p) n -> p c n", p=P)
    FB = [(f0, min(512, FH - f0)) for f0 in range(0, FH, 512)]
    PREVT = k.SCRF[:, 3328:3328 + 704].rearrange("p (c j s) -> p c j s", c=FC, j=2)
    GRB = k.RSTD
    S.dma("sp", O["conv_s"][li][:, 0:FH], I["st_conv"][li][:, FH:2 * FH])
    nblk = 2 * FC
    for q in range(0, nblk, 8):
        nb = min(8, nblk - q)
        S.dma("sp", k.TOK[0:NS, 0:nb * P], I["st_conv"][li][:, q * P:(q + nb) * P])
        pt = k.ps()
        for b in range(nb):
            S.I("pe", "transpose", out=pt[:, b * NS:(b + 1) * NS], in_=k.TOK[0:NS, b * P:(b + 1) * P], identity=k.ident_f[0:NS, 0:NS])
        for b in range(nb):
            j, fc = divmod(q + b, FC)
            S.I("act", "activation", out=PREVT[:, fc, j, :], in_=pt[:, b * NS:(b + 1) * NS], func=AF.Copy)

    def gbuf(par):
        o = par * 1664
        return (k.SCRF[:, o:o + 514], k.SCRF[:, o + 514:o + 1026], k.SCRF[:, o + 1026:o + 1538])

    unit = 0
    for hi, tiles in enumerate(HALVES):
        ncols = sum(n for _, n in tiles)
        c0 = tiles[0][0]
        for bi, (f0, fw) in enumerate(FB):
            par = bi % 2
            Wg = k.WA[:, par * 8192:par * 8192 + 4096].rearrange("p (c n) -> p c n", c=DC)
            Wu = k.WA[:, par * 8192 + 4096:par * 8192 + 8192].rearrange("p (c n) -> p c n", c=DC)
            S.dma("pool", Wg[:, :, 0:fw], w_in[:, :, f0:f0 + fw])
            S.dma("pool", Wu[:, :, 0:fw], w_in[:, :, FH + f0:FH + f0 + fw])
            for j in range(fw // P):
                fc = f0 // P + j
                for (n0, n) in tiles:
                    G, U, C = gbuf(unit % 2)
                    unit += 1
                    pg, pu = k.ps(), k.ps()
                    for c in range(DC):
                        S.I("pe", "matmul", out=pg[:, 0:n], lhsT=Wg[:, c, j * P:(j + 1) * P], rhs=k.HT[:, c, n0:n0 + n],
                            start=(c == 0), stop=(c == DC - 1))
                    for c in range(DC):
                        S.I("pe", "matmul", out=pu[:, 0:n], lhsT=Wu[:, c, j * P:(j + 1) * P], rhs=k.HT[:, c, n0:n0 + n],
                            start=(c == 0), stop=(c == DC - 1))
                    w0 = k.cw[:, li, 0, fc:fc + 1]
                    w1 = k.cw[:, li, 1, fc:fc + 1]
                    w2 = k.cw[:, li, 2, fc:fc + 1]
                    bb = k.cb[:, li, fc:fc + 1]
                    hc = n0 - c0
                    hid = k.HID[:, fc * 1040 + hc:fc * 1040 + hc + n]
                    if n0 < TP:
                        if n0 == 0:
                            S.I("pool", "memset", ap=G[:, 0:2], constant=0.0)
                        else:
                            S.I("pool", "tensor_copy", out=G[:, 0:2], in_=k.GT[:, fc, :])
                        S.I("act", "activation", out=G[:, 2:2 + n], in_=pg[:, 0:n], func=AF.Copy)
                        S.I("pool", "tensor_copy", out=k.GT[:, fc, :], in_=G[:, n:n + 2])
                        S.I("dve", "tensor_scalar", out=C[:, 0:n], in0=G[:, 2:2 + n], scalar1=w2, scalar2=bb, op0=ALU.mult, op1=ALU.add)
                        S.I("dve", "scalar_tensor_tensor", out=C[:, 0:n], in0=G[:, 1:1 + n], scalar=w1, in1=C[:, 0:n], op0=ALU.mult, op1=ALU.add)
                        S.I("dve", "scalar_tensor_tensor", out=C[:, 0:n], in0=G[:, 0:n], scalar=w0, in1=C[:, 0:n], op0=ALU.mult, op1=ALU.add)
                    else:
                        S.I("dve", "tensor_scalar", out=C[:, 0:n], in0=pg[:, 0:n], scalar1=w2, scalar2=bb, op0=ALU.mult, op1=ALU.add)
                        S.I("dve", "scalar_tensor_tensor", out=C[:, 0:n], in0=PREVT[:, fc, 1, :], scalar=w1, in1=C[:, 0:n], op0=ALU.mult, op1=ALU.add)
                        S.I("dve", "scalar_tensor_tensor", out=C[:, 0:n], in0=PREVT[:, fc, 0, :], scalar=w0, in1=C[:, 0:n], op0=ALU.mult, op1=ALU.add)
                    S.I("act", "activation", out=C[:, 0:n], in_=C[:, 0:n], func=GELU)
                    S.I("dve", "tensor_tensor", out=hid, in0=C[:, 0:n], in1=pu[:, 0:n], op=ALU.mult)
            if hi == 1:
                pr = k.ps()
                for c in range(DC):
                    S.I("pe", "matmul", out=pr[0:18, 0:fw], lhsT=k.HT[:, c, TP - 2:NT], rhs=Wg[:, c, 0:fw],
                        start=(c == 0), stop=(c == DC - 1))
                S.I("act", "activation", out=GRB[0:18, 0:fw], in_=pr[0:18, 0:fw], func=AF.Copy)
                S.dma("sp", O["conv_p"][li][:, f0:f0 + fw], GRB[0:2, 0:fw])
                S.dma("sp", O["conv_s"][li][:, FH + f0:FH + f0 + fw], GRB[2:18, 0:fw])
        for mb in range(4):
            par = mb % 2
            Wo = k.WA[:, par * 8192:par * 8192 + FC * 256].rearrange("p (c n) -> p c n", c=FC)
            S.dma("pool", Wo[:, 0:11, :], w_out[:, 0:11, mb * 256:(mb + 1) * 256])
            S.dma("pool", Wo[:, 11:22, :], w_out[:, 11:22, mb * 256:(mb + 1) * 256])
            for (n0, n) in tiles:
                hc = n0 - c0
                for mm in range(2):
                    m = mb * 2 + mm
                    po = k.ps()
                    for c in range(FC):
                        S.I("pe", "matmul", out=po[:, 0:n], lhsT=Wo[:, c, mm * P:(mm + 1) * P],
                            rhs=k.HID[:, c * 1040 + hc:c * 1040 + hc + n], start=(c == 0), stop=(c == FC - 1))
                    S.I("dve", "tensor_tensor", out=k.XT[:, m, n0:n0 + n], in0=po[:, 0:n], in1=k.XT[:, m, n0:n0 + n], op=ALU.add)


def final_norm(k):
    S = k.S
    HF = k.HID[:, 0:DC * 512]
    for (n0, n) in TT:
        S.I("act", "activation", out=k.SQ[:, :, 0:n], in_=k.XT[:, :, n0:n0 + n], func=AF.Square)
        pt = k.ps()
        for c in range(DC):
            S.I("pe", "matmul", out=pt[:, 0:n], lhsT=k.ones_bf[:], rhs=k.SQ[:, c, 0:n], start=(c == 0), stop=(c == DC - 1))
        S.I("act", "activation", out=k.RSTD[:, 0:n], in_=pt[:, 0:n], func=AF.Sqrt, scale=1.0 / D, bias=k.small[:, 0:1])
        S.I("dve", "reciprocal", out=k.RSTD[:, 0:n], in_=k.RSTD[:, 0:n])
        YT = k.SCRF[:, 0:4096].rearrange("p (c n) -> p c n", c=DC)
        for c in range(DC):
            S.I("dve", "scalar_tensor_tensor", out=YT[:, c, 0:n], in0=k.XT[:, c, n0:n0 + n],
                scalar=k.gfin[:, c:c + 1], in1=k.RSTD[:, 0:n], op0=ALU.mult, op1=ALU.mult)
        for b0 in range(0, n, P):
            nb = min(P, n - b0)
            buf = k.TOK if (b0 // P) % 2 == 0 else k.TOK2
            for half in range(2):
                pt2 = k.ps()
                for j in range(4):
                    c = half * 4 + j
                    S.I("pe", "transpose", out=pt2[0:nb, j * P:(j + 1) * P], in_=YT[:, c, b0:b0 + nb], identity=k.ident_f[:])
                if half == 0:
                    S.I("act", "activation", out=buf[0:nb, 0:512], in_=pt2[0:nb, :], func=AF.Copy)
                else:
                    S.I("dve", "tensor_copy", out=buf[0:nb, 512:1024], in_=pt2[0:nb, :])
            if n0 < TP:
                S.dma("sp", k.O["y_p"][n0 + b0:n0 + b0 + nb, :], buf[0:nb, :])
            else:
                S.dma("sp", k.O["y_s"], buf[0:nb, :])


_PROG = {}


def _get_prog():
    if "nc" not in _PROG:
        _PROG["nc"] = build_program()
    return _PROG["nc"]


def _t5_onehot():
    import math
    def bucket(rel):
        n = max(rel, 0)
        if n < 16:
            return n
        v = np.float32(np.log(np.float32(max(n, 1)) / np.float32(16)) / np.float32(math.log(128 / 16)) * np.float32(16))
        return min(16 + int(v), 31)
    oh = np.zeros((33, 512), np.float32)
    for m in range(256):
        if m <= 127:
            oh[bucket(127 - m), m] = 1.0
        else:
            oh[32, m] = 1.0
    for u in range(256):
        if u <= 254:
            oh[bucket(255 - u), 256 + u] = 1.0
        else:
            oh[32, 256 + u] = 1.0
    return oh


def _t5_onehot15():
    full = _t5_onehot()
    oh = np.zeros((32, 128), np.float32)
    for key in range(128):
        rel = 128 - key
        colv = full[:32, 127 - rel] if rel <= 127 else full[:32, 256 + 255 - rel]
        oh[:, key] = colv
    return oh


def _rw_vecs(inp):
    f = lambda a: np.ascontiguousarray(np.asarray(a, dtype=np.float32))
    mu = f(inp["rwkv_mu"])[0]
    rows = [mu[i] for i in range(6)] + [f(inp["rwkv_w0"])[0], f(inp["rwkv_a0"])[0], f(inp["rwkv_k_k"])[0], f(inp["rwkv_k_a"])[0],
                                         f(inp["rwkv_r_k"])[0].reshape(-1), f(inp["rwkv_lnx_g"])[0], f(inp["rwkv_lnx_b"])[0]]
    return np.ascontiguousarray(np.stack(rows).reshape(13 * DC, P))


def kernel(**inp):
    f = lambda a: np.ascontiguousarray(np.asarray(a, dtype=np.float32))
    nc = _get_prog()
    shared = dict(
        norm_mix_g=f(inp["norm_mix_g"]), norm_ffn_g=f(inp["norm_ffn_g"]),
        norm_final_g=f(inp["norm_final_g"]).reshape(1, D),
        gm_w_in=f(inp["gm_w_in"])[0], gm_ln_g=f(inp["gm_ln_g"]), gm_ln_b=f(inp["gm_ln_b"]),
        gm_w_s=f(inp["gm_w_s"])[0], gm_b_s=f(inp["gm_b_s"])[0], gm_w_out=f(inp["gm_w_out"])[0],
        ffn_w_in=f(inp["ffn_w_in"]), ffn_conv_w=f(inp["ffn_conv_w"]), ffn_conv_b=f(inp["ffn_conv_b"]),
        ffn_w_out=f(inp["ffn_w_out"]),
        t5_oh=_t5_onehot(), t5_oh15=_t5_onehot15(), rw_vecs=_rw_vecs(inp), rwkv_lnx=np.stack([f(inp["rwkv_lnx_g"])[0], f(inp["rwkv_lnx_b"])[0]]),
        rwkv_w_r=f(inp["rwkv_w_r"])[0], rwkv_w_k=f(inp["rwkv_w_k"])[0], rwkv_w_v=f(inp["rwkv_w_v"])[0], rwkv_w_o=f(inp["rwkv_w_o"])[0],
        rwkv_w1=f(inp["rwkv_w1"])[0], rwkv_a1=f(inp["rwkv_a1"])[0], rwkv_g1=f(inp["rwkv_g1"])[0],
        rwkv_w2=f(inp["rwkv_w2"])[0], rwkv_a2=f(inp["rwkv_a2"])[0], rwkv_g2=f(inp["rwkv_g2"])[0],
        pool_w=f(inp["pool_w"])[0], pool_scale=f(inp["pool_scale"]), moba_w_qkv=f(inp["moba_w_qkv"])[0], moba_w_o=f(inp["moba_w_o"])[0], rel_bias=f(inp["rel_bias"]),
    )
    xp = f(inp["x_prompt"])
    xs = f(inp["x_sample"])
    stc = f(inp["state_ffn_conv"])
    if WITH_CACHE:
        ck = f(inp["cache_moba_k"]).reshape(-1, D)
        cv = f(inp["cache_moba_v"]).reshape(-1, D)
    pt_all = np.ascontiguousarray(np.asarray(inp["page_table"], dtype=np.int32))
    in_maps = []
    for c in range(8):
        m = dict(shared)
        if WITH_CACHE:
            m["cache_k"] = ck
            m["cache_v"] = cv
        m["st_shift"] = f(inp["state_rwkv_shift"])[0, c * NS:(c + 1) * NS]
        m["st_wkv"] = f(inp["state_rwkv_wkv"])[0, c * NS:(c + 1) * NS]
        m["st_pool"] = f(inp["state_pool"])[0, c * NS:(c + 1) * NS]
        m["page_table"] = pt_all[c * NS:(c + 1) * NS].reshape(1, NS * 16)
        m["xp"] = xp[c]
        m["xs"] = xs[c * NS:(c + 1) * NS, 0]
        m["st_conv"] = np.ascontiguousarray(stc[:, c * NS:(c + 1) * NS].reshape(4, NS, 2 * FH))
        in_maps.append(m)
    res = run_bass_kernel_spmd(nc, in_maps, core_ids=list(range(8)))
    R = res.results
    cat = lambda key: np.stack([R[c][key] for c in range(8)])
    y_prompt = cat("y_p")
    y_sample = np.concatenate([R[c]["y_s"] for c in range(8)])[:, None, :]
    gm_v = np.concatenate([R[c]["gm_v"] for c in range(8)])[None, :, None, :]
    conv_p = np.stack([R[c]["conv_p"] for c in range(8)], axis=1)
    conv_s = np.concatenate([R[c]["conv_s"].reshape(4, NS, 2, FH) for c in range(8)], axis=1)
    z = lambda *s: np.zeros(s, np.float32)
    mk_p = cat("mk_p").reshape(1, 8, TP, 16, 64)
    mv_p = cat("mv_p").reshape(1, 8, TP, 16, 64)
    mk_s = np.concatenate([R[c]["mk_s"] for c in range(8)]).reshape(1, 128, 1, 16, 64)
    mv_s = np.concatenate([R[c]["mv_s"] for c in range(8)]).reshape(1, 128, 1, 16, 64)
    pool_p = cat("pool_p")[None]
    pool_s = np.concatenate([R[c]["pool_s"] for c in range(8)])[None]
    return (y_prompt, y_sample, gm_v, mk_p, mv_p, mk_s, mv_s,
            pool_p, pool_s, cat("wkv_p")[None], np.concatenate([R[c]["wkv_s"] for c in range(8)])[None],
            np.stack([R[c]["shift_p"][0] for c in range(8)])[None], np.concatenate([R[c]["shift_s"] for c in range(8)])[None],
            conv_p, conv_s)
```
